# Optimizing a Trainium2 kernel written in Bass

```python
import math
import jax, jax.numpy as jnp
from jax import lax
import numpy as np

D_MODEL = 1024
BATCH = 8
SEQ = 4096
DEPTH = 2

N_MIXERS = 2
HEAD_DIM_A = 64
DIL_GROUPS = ((128, 1), (512, 4), (2048, 16))
HEADS_PER_GROUP_A = 6
N_HEADS_A = HEADS_PER_GROUP_A * len(DIL_GROUPS)
WIDTH_A = N_HEADS_A * HEAD_DIM_A
N_BUCKETS = 32
T5_MAX_DISTANCE = 1024
N_HEADS_B = 16
QK_NOPE_DIM = 64
QK_ROPE_DIM = 32
V_HEAD_DIM = 64
Q_LORA_RANK = 256
KV_LORA_RANK = 128
WIDTH_B = N_HEADS_B * V_HEAD_DIM
IN_B = Q_LORA_RANK + KV_LORA_RANK + QK_ROPE_DIM + WIDTH_B
ROPE_THETA = 10000.0
Q_BLOCK = 128
DEPTH_A = (DEPTH + N_MIXERS - 1) // N_MIXERS
DEPTH_B = DEPTH // N_MIXERS
DEEPNORM_ALPHA = (2.0 * DEPTH) ** 0.25
DEEPNORM_BETA = (8.0 * DEPTH) ** -0.25
LN_EPS = 1e-5
RMS_EPS = 1e-6
NEG_INF = -1e30

kernel_name = "hybrid_dilated_mla_encoder"


def layernorm(x, g, b):
    xf = x.astype(jnp.float32)
    mu = xf.mean(-1, keepdims=True)
    var = jnp.square(xf - mu).mean(-1, keepdims=True)
    return ((xf - mu) * lax.rsqrt(var + LN_EPS) * g.astype(jnp.float32) + b.astype(jnp.float32)).astype(x.dtype)


def rmsnorm(x, g):
    xf = x.astype(jnp.float32)
    return (xf * lax.rsqrt(jnp.square(xf).mean(-1, keepdims=True) + RMS_EPS) * g.astype(jnp.float32)).astype(x.dtype)


def t5_bucket(rel):
    half = N_BUCKETS // 2
    max_exact = half // 2
    base = jnp.where(rel > 0, half, 0)
    n = jnp.abs(rel)
    nf = jnp.maximum(n, 1).astype(jnp.float32)
    large = max_exact + (jnp.log(nf / max_exact) / math.log(T5_MAX_DISTANCE / max_exact)
                         * (half - max_exact)).astype(jnp.int32)
    large = jnp.minimum(large, half - 1)
    return base + jnp.where(n < max_exact, n, large)


def dilated_window_attention(q, k, v, bias_table, window, dil):
    B, S, H, Dh = q.shape
    R = window // (2 * dil)
    L = S // dil
    nb = -(-L // R)
    Lp = nb * R

    def to_sub(t):
        return t.reshape(B, L, dil, H, Dh).transpose(0, 2, 3, 1, 4)

    qs = jnp.pad(to_sub(q), ((0, 0), (0, 0), (0, 0), (0, Lp - L), (0, 0))).reshape(B, dil, H, nb, R, Dh)

    def key_blocks(t):
        tp = jnp.pad(to_sub(t), ((0, 0), (0, 0), (0, 0), (R, Lp - L + R), (0, 0))).reshape(B, dil, H, nb + 2, R, Dh)
        return jnp.concatenate([tp[:, :, :, :-2], tp[:, :, :, 1:-1], tp[:, :, :, 2:]], axis=4)

    kb = key_blocks(k)
    vb = key_blocks(v)
    a = jnp.arange(R)[:, None]
    bk = jnp.arange(3 * R)[None, :]
    off = bk - R - a
    kpos = jnp.arange(nb)[:, None, None] * R + bk[None] - R
    valid = (jnp.abs(off) <= R)[None] & (kpos >= 0) & (kpos < L)
    bias = bias_table[t5_bucket(off * dil)].transpose(2, 0, 1).astype(jnp.float32)

    s = jnp.einsum('bdhnqe,bdhnke->bdhnqk', qs, kb).astype(jnp.float32) * (Dh ** -0.5) + bias[:, None]
    s = jnp.where(valid, s, NEG_INF)
    m = s.max(-1, keepdims=True)
    p = jnp.exp(s - m)
    denom = p.sum(-1, keepdims=True)
    o = jnp.einsum('bdhnqk,bdhnke->bdhnqe', p, vb.astype(jnp.float32)) / denom
    lse = (m + jnp.log(denom))[..., 0]

    def from_sub(t):
        t = t.reshape(B, dil, H, Lp, *t.shape[5:])[:, :, :, :L]
        t = jnp.moveaxis(t, 3, 1)
        return t.reshape(B, S, H, *t.shape[4:])

    return from_sub(o).astype(q.dtype), from_sub(lse)


def mixer_dilated(u, w_in, w_out, rel_bias):
    B, S, _ = u.shape
    q, k, v, gate = jnp.split(u @ w_in, 4, axis=-1)
    q = q.reshape(B, S, N_HEADS_A, HEAD_DIM_A)
    k = k.reshape(B, S, N_HEADS_A, HEAD_DIM_A)
    v = v.reshape(B, S, N_HEADS_A, HEAD_DIM_A)
    outs, lses = [], []
    for g, (window, dil) in enumerate(DIL_GROUPS):
        hs = slice(g * HEADS_PER_GROUP_A, (g + 1) * HEADS_PER_GROUP_A)
        o, l = dilated_window_attention(q[:, :, hs], k[:, :, hs], v[:, :, hs], rel_bias[:, hs], window, dil)
        outs.append(o)
        lses.append(l)
    wts = jax.nn.softmax(jnp.stack(lses, axis=0), axis=0).astype(u.dtype)
    o = jnp.concatenate([outs[g] * wts[g][..., None] for g in range(len(DIL_GROUPS))], axis=2)
    y = o.reshape(B, S, WIDTH_A) * jax.nn.silu(gate)
    return y @ w_out


def rope(x, cos, sin):
    half = x.shape[-1] // 2
    x1, x2 = x[..., :half], x[..., half:]
    return jnp.concatenate([x1 * cos - x2 * sin, x2 * cos + x1 * sin], axis=-1)


def mixer_mla(u, w_in, q_norm, w_uq, kv_norm, w_ukv, w_out):
    B, S, _ = u.shape
    cq, ckv, k_rope, gate = jnp.split(
        u @ w_in, [Q_LORA_RANK, Q_LORA_RANK + KV_LORA_RANK, Q_LORA_RANK + KV_LORA_RANK + QK_ROPE_DIM], axis=-1)
    q = (rmsnorm(cq, q_norm) @ w_uq).reshape(B, S, N_HEADS_B, QK_NOPE_DIM + QK_ROPE_DIM)
    q_nope, q_rope = q[..., :QK_NOPE_DIM], q[..., QK_NOPE_DIM:]
    kv = (rmsnorm(ckv, kv_norm) @ w_ukv).reshape(B, S, N_HEADS_B, QK_NOPE_DIM + V_HEAD_DIM)
    k_nope, v = kv[..., :QK_NOPE_DIM], kv[..., QK_NOPE_DIM:]

    pos = jnp.arange(S, dtype=jnp.float32)
    inv_freq = ROPE_THETA ** (-jnp.arange(0, QK_ROPE_DIM, 2, dtype=jnp.float32) / QK_ROPE_DIM)
    ang = pos[:, None] * inv_freq[None, :]
    cos, sin = jnp.cos(ang).astype(u.dtype), jnp.sin(ang).astype(u.dtype)
    q_rope = rope(q_rope, cos[:, None], sin[:, None])
    k_rope = rope(k_rope, cos, sin)

    scale = (QK_NOPE_DIM + QK_ROPE_DIM) ** -0.5
    nq = S // Q_BLOCK
    qn_b = q_nope.reshape(B, nq, Q_BLOCK, N_HEADS_B, QK_NOPE_DIM).transpose(1, 0, 2, 3, 4)
    qr_b = q_rope.reshape(B, nq, Q_BLOCK, N_HEADS_B, QK_ROPE_DIM).transpose(1, 0, 2, 3, 4)

    def block(args):
        qn, qr = args
        s = (jnp.einsum('bqhd,bkhd->bhqk', qn, k_nope)
             + jnp.einsum('bqhd,bkd->bhqk', qr, k_rope)).astype(jnp.float32) * scale
        p = jax.nn.softmax(s, axis=-1).astype(v.dtype)
        return jnp.einsum('bhqk,bkhd->bqhd', p, v)

    o = lax.map(block, (qn_b, qr_b))
    o = o.transpose(1, 0, 2, 3, 4).reshape(B, S, WIDTH_B)
    y = o * jax.nn.silu(gate)
    return y @ w_out


def setup_inputs(seed: int = 0) -> dict:
    key = jax.random.key(seed)
    ks = jax.random.split(key, 20)
    nrm = jax.random.normal
    D = D_MODEL
    x = nrm(ks[0], (BATCH, SEQ, D), jnp.float32)
    c = nrm(ks[1], (BATCH, D), jnp.float32)
    rel_bias = 0.2 * nrm(ks[2], (N_BUCKETS, N_HEADS_A), jnp.float32)
    ada_w = 0.5 * D ** -0.5 * nrm(ks[3], (DEPTH, D, 3 * D), jnp.float32)
    ada_b = 0.02 * nrm(ks[4], (DEPTH, 3 * D), jnp.float32)
    ln_g = 1.0 + 0.05 * nrm(ks[5], (DEPTH, D), jnp.float32)
    ln_b = 0.02 * nrm(ks[6], (DEPTH, D), jnp.float32)
    a_w_in = D ** -0.5 * nrm(ks[7], (DEPTH_A, D, 4 * WIDTH_A), jnp.float32)
    a_w_in = a_w_in.at[:, :, 2 * WIDTH_A:3 * WIDTH_A].multiply(DEEPNORM_BETA)
    a_w_out = DEEPNORM_BETA * WIDTH_A ** -0.5 * nrm(ks[8], (DEPTH_A, WIDTH_A, D), jnp.float32)
    b_w_in = D ** -0.5 * nrm(ks[9], (DEPTH_B, D, IN_B), jnp.float32)
    b_q_norm = 1.0 + 0.05 * nrm(ks[10], (DEPTH_B, Q_LORA_RANK), jnp.float32)
    b_w_uq = Q_LORA_RANK ** -0.5 * nrm(ks[11], (DEPTH_B, Q_LORA_RANK, N_HEADS_B * (QK_NOPE_DIM + QK_ROPE_DIM)), jnp.float32)
    b_kv_norm = 1.0 + 0.05 * nrm(ks[12], (DEPTH_B, KV_LORA_RANK), jnp.float32)
    b_w_ukv = KV_LORA_RANK ** -0.5 * nrm(ks[13], (DEPTH_B, KV_LORA_RANK, N_HEADS_B, QK_NOPE_DIM + V_HEAD_DIM), jnp.float32)
    b_w_ukv = b_w_ukv.at[..., QK_NOPE_DIM:].multiply(DEEPNORM_BETA).reshape(
        DEPTH_B, KV_LORA_RANK, N_HEADS_B * (QK_NOPE_DIM + V_HEAD_DIM))
    b_w_out = DEEPNORM_BETA * WIDTH_B ** -0.5 * nrm(ks[14], (DEPTH_B, WIDTH_B, D), jnp.float32)
    return {"x": x, "c": c, "rel_bias": rel_bias, "ada_w": ada_w, "ada_b": ada_b,
            "ln_g": ln_g, "ln_b": ln_b, "a_w_in": a_w_in, "a_w_out": a_w_out,
            "b_w_in": b_w_in, "b_q_norm": b_q_norm, "b_w_uq": b_w_uq, "b_kv_norm": b_kv_norm,
            "b_w_ukv": b_w_ukv, "b_w_out": b_w_out}


def reference(x, c, rel_bias, ada_w, ada_b, ln_g, ln_b, a_w_in, a_w_out,
              b_w_in, b_q_norm, b_w_uq, b_kv_norm, b_w_ukv, b_w_out):
    for i in range(DEPTH):
        mod = jax.nn.silu(c) @ ada_w[i] + ada_b[i]
        shift, scale, gate = jnp.split(mod[:, None, :], 3, axis=-1)
        u = x * (1 + scale) + shift
        j = i // N_MIXERS
        if i % N_MIXERS == 0:
            y = mixer_dilated(u, a_w_in[j], a_w_out[j], rel_bias)
        else:
            y = mixer_mla(u, b_w_in[j], b_q_norm[j], b_w_uq[j], b_kv_norm[j], b_w_ukv[j], b_w_out[j])
        x = layernorm(DEEPNORM_ALPHA * x + gate * y, ln_g[i], ln_b[i])
    return x
```

```python
import contextlib
import numpy as np
import concourse.bass as bass
import concourse.mybir as mybir
from concourse.bass_utils import run_bass_kernel_spmd

F32 = mybir.dt.float32
BF16 = mybir.dt.bfloat16
AF = mybir.ActivationFunctionType
ALU = mybir.AluOpType
AX = mybir.AxisListType


class Prog:
    ENG = ("pe", "act", "dve", "pool", "sp")

    def __init__(self, nc, stack):
        self.nc = nc
        self.stack = stack
        self.stream = {e: [] for e in self.ENG}
        self.semh = {}
        self.semcnt = {}
        self.waited = {e: {} for e in self.ENG}
        self.res = {}
        self.nbuf = 0
        self.self_wait = True
        self.barrier_req = {e: {} for e in self.ENG}

    def init_arena(self, nbytes):
        self.stack.enter_context(self.nc.sbuf_tensor("arena", [128, nbytes // 2], BF16))
        self.a0 = int(self.nc.sbuf_base) - nbytes
        self.aoff = 0
        self.acap = nbytes

    def sb(self, name, shape, dt):
        n = 1
        for s in shape[1:]:
            n *= int(s)
        nb = n * (4 if dt == F32 else 2)
        nb = (nb + 63) // 64 * 64
        off = self.aoff
        self.aoff += nb
        assert self.aoff <= self.acap, ("SBUF arena overflow", name, self.aoff, self.acap)
        return self.nc.alloc_sbuf_tensor_at(name, list(shape), dt, offset=self.a0 + off)

    def mark(self):
        return self.aoff

    def release(self, m):
        for e in self.ENG:
            req = self.barrier_req[e]
            for s, c in self.semcnt.items():
                if c > 0 and req.get(s, 0) < c:
                    req[s] = c
        self.aoff = m

    def ps(self, name, shape, dt=F32):
        return self.stack.enter_context(self.nc.psum_tensor(name, list(shape), dt))

    def dram(self, name, shape, dt, kind="Internal"):
        return self.nc.dram_tensor(name, list(shape), dt, kind=kind).ap()

    def _sem(self, name):
        if name not in self.semh:
            self.semh[name] = self.stack.enter_context(self.nc.semaphore(name))
            self.semcnt[name] = 0
        return name

    def rec_start(self):
        self._rec = []

    def rec_stop(self):
        r = self._rec
        self._rec = None
        return r

    def emit(self, lst):
        for a in lst:
            self.op(*a)

    def emit_zip(self, a, b):
        i = j = 0
        na, nb = len(a), len(b)
        while i < na or j < nb:
            if j >= nb or (i < na and i * nb <= j * na):
                self.op(*a[i])
                i += 1
            else:
                self.op(*b[j])
                j += 1

    def emit_zipn(self, lists):
        lists = [l for l in lists if l]
        idx = [0] * len(lists)
        while True:
            best, bf_ = -1, 2.0
            for k, l in enumerate(lists):
                if idx[k] < len(l):
                    f = idx[k] / len(l)
                    if f < bf_:
                        best, bf_ = k, f
            if best < 0:
                break
            self.op(*lists[best][idx[best]])
            idx[best] += 1

    def op(self, eng, fn, reads=(), writes=(), inc=True, chan=None):
        if getattr(self, "_rec", None) is not None:
            self._rec.append((eng, fn, tuple(reads), tuple(writes), inc, chan))
            return None
        is_dma = chan is not None
        deps = {}
        writes = list(writes) + [r for r in reads if isinstance(r, tuple) and r[0] in ("pX", "pY", "pZ")]

        def need(t):
            if t is None:
                return
            s, v = t
            if s == "c_" + eng and (eng == "pe" or not self.self_wait):
                return
            if deps.get(s, 0) < v:
                deps[s] = v

        for r in reads:
            st = self.res.get(r)
            if st is not None:
                for t in st["w"].items():
                    need(t)
        for w in writes:
            st = self.res.get(w)
            if st is not None:
                for t in st["w"].items():
                    need(t)
                for t in st["r"].items():
                    need(t)
        if self.barrier_req[eng]:
            for s, v in self.barrier_req[eng].items():
                if s != "c_" + eng and deps.get(s, 0) < v:
                    deps[s] = v
            self.barrier_req[eng] = {}
        waits = []
        wd = self.waited[eng]
        for s, v in deps.items():
            if wd.get(s, 0) < v:
                wd[s] = v
                waits.append((s, v))
        if is_dma:
            s = self._sem("d_" + chan)
            self.semcnt[s] += 16
            tick = (s, self.semcnt[s])
            incinfo = (s, 16)
        else:
            s = self._sem("c_" + eng)
            if inc:
                self.semcnt[s] += 1
                tick = (s, self.semcnt[s])
                incinfo = (s, 1)
            else:
                tick = (s, self.semcnt[s] + 1)
                incinfo = None
        self.stream[eng].append((waits, fn, incinfo))
        for r in reads:
            st = self.res.setdefault(r, {"w": {}, "r": {}})
            if st["r"].get(tick[0], 0) < tick[1]:
                st["r"][tick[0]] = tick[1]
        for w in writes:
            st = self.res.setdefault(w, {"w": {}, "r": {}})
            if st["w"].get(tick[0], 0) < tick[1]:
                st["w"][tick[0]] = tick[1]
        return tick

    def mm(self, out, lhsT, rhs, start, stop, reads=(), writes=(), inc=None, **kw):
        if inc is None:
            inc = bool(stop)
        return self.op("pe", lambda e: e.matmul(out, lhsT, rhs, start=start, stop=stop, **kw),
                       reads, writes, inc=inc)

    def tr(self, out, in_, ident, reads=(), writes=(), inc=True):
        return self.op("pe", lambda e: e.transpose(out, in_, ident), reads, writes, inc=inc)

    def act(self, out, in_, func, reads=(), writes=(), eng="act", **kw):
        return self.op(eng, lambda e: e.activation(out, in_, func, **kw), reads, writes)

    def dma(self, out, in_, chan, reads=(), writes=(), eng="sp", **kw):
        return self.op(eng, lambda e: e.dma_start(out=out, in_=in_, **kw), reads, writes, chan=chan)

    def finish(self, final_waits=()):
        fw = {}
        for r in final_waits:
            st = self.res.get(r)
            if st and st["w"]:
                for s, v in st["w"].items():
                    fw[s] = max(fw.get(s, 0), v)
        for s, c in self.semcnt.items():
            if s.startswith("d_") and c > 0:
                fw[s] = max(fw.get(s, 0), c)
        nc = self.nc
        semh = self.semh
        streams = self.stream

        def run(e, key):
            for waits, fn, incinfo in streams[key]:
                for s, v in waits:
                    e.wait_ge(semh[s], v)
                ins = fn(e)
                if incinfo is not None:
                    ins.then_inc(semh[incinfo[0]], incinfo[1])

        with nc.Block() as block:
            @block.tensor
            def _(e):
                run(e, "pe")

            @block.scalar
            def _(e):
                run(e, "act")

            @block.vector
            def _(e):
                run(e, "dve")

            @block.gpsimd
            def _(e):
                run(e, "pool")

            @block.sync
            def _(e):
                run(e, "sp")
                for s, v in fw.items():
                    e.wait_ge(semh[s], v)


import math

S = 4096
D = 1024
NT = 32
WA = 1152
WB = 1024
ALPHA = float((2.0 * 2) ** 0.25)
GROUPS = ((128, 1), (512, 4), (2048, 16))


def _t5_bucket_np(rel):
    half, max_exact = 16, 8
    base = np.where(rel > 0, half, 0)
    n = np.abs(rel)
    nf = np.maximum(n, 1).astype(np.float32)
    large = max_exact + (np.log(nf / np.float32(max_exact)) / np.float32(math.log(1024 / max_exact))
                         * np.float32(half - max_exact)).astype(np.int32)
    large = np.minimum(large, half - 1)
    return base + np.where(n < max_exact, n, large)


def _const_tables():
    i = np.arange(128)[:, None]
    j = np.arange(128)[None, :]
    offA = i - 64 - j
    offB = 64 + i - j
    mask = np.zeros((2, 128, 128), np.float32)
    mask[0][~(i >= j)] = -240000.0
    mask[1][~(i <= j)] = -240000.0
    bidx = np.zeros((3, 2, 128, 128), np.int64)
    for g, (_, dil) in enumerate(GROUPS):
        bidx[g, 0] = _t5_bucket_np(np.clip(offA, -64, 64) * dil)
        bidx[g, 1] = _t5_bucket_np(np.clip(offB, -64, 64) * dil)
    pos = np.arange(S, dtype=np.float32)
    inv_freq = (np.float32(10000.0) ** (-np.arange(0, 32, 2, dtype=np.float32) / np.float32(32))).astype(np.float32)
    ang = (pos[:, None] * inv_freq[None, :]).astype(np.float32)
    cos = np.cos(ang).astype(np.float32)
    sin = np.sin(ang).astype(np.float32)
    return mask, bidx, cos, sin


def build_program(debug=False, stop_after="C1"):
    nc = bass.Bass("TRN2", target_bir_lowering=False)
    order = ["A0", "B0", "C0", "A1", "B1", "C1"]
    nph = order.index(stop_after) + 1
    phases = set(order[:nph])

    def din(name, shape, dt=F32):
        return nc.dram_tensor(name, list(shape), dt, kind="ExternalInput").ap()

    x_in = din("x", [S, D])
    cT_in = din("cT", [128, 8])
    adaw_in = din("ada_w", [2, D, 3 * D])
    adab_in = din("ada_b", [2, 3 * D])
    lng_in = din("ln_g", [2, D])
    lnb_in = din("ln_b", [2, D])
    awin_in = din("a_w_in", [D, 4 * WA])
    awout_in = din("a_w_out", [WA, D])
    bwin_in = din("b_w_in", [D, 1440])
    bqn_in = din("b_q_norm", [256])
    bwuq_in = din("b_w_uq", [256, 1536])
    bkvn_in = din("b_kv_norm", [128])
    bwukv_in = din("b_w_ukv", [128, 2048])
    bwout_in = din("b_w_out", [WB, D])
    ident_in = din("ident", [128, 128])
    cos_in = din("cosT", [128, NT, 16])
    sin_in = din("sinT", [128, NT, 16])
    mask_in = din("maskT", [128, 2, 128])
    btab_in = din("btab", [128, 36, 128])

    okind = "ExternalOutput"
    skind = "ExternalOutput" if debug else "Internal"
    out = nc.dram_tensor("out", [S, D], F32, kind=okind).ap()
    QKV0 = nc.dram_tensor("s_qkv0", [S, 3 * WA], BF16, kind=skind).ap()
    SG0 = nc.dram_tensor("s_sg0", [S, WA], BF16, kind=skind).ap()
    ATT0 = nc.dram_tensor("s_att0", [S, 18 * 65], F32, kind=skind).ap()
    X1 = nc.dram_tensor("s_x1", [S, D], F32, kind=skind).ap()
    Q1 = nc.dram_tensor("s_q1", [S, 16 * 96], BF16, kind=skind).ap()
    K1 = nc.dram_tensor("s_k1", [S, 16 * 96], BF16, kind=skind).ap()
    V1 = nc.dram_tensor("s_v1", [S, 16 * 64], BF16, kind=skind).ap()
    SG1 = nc.dram_tensor("s_sg1", [S, WB], BF16, kind=skind).ap()
    ATT1 = nc.dram_tensor("s_att1", [S, 16 * 65], F32, kind=skind).ap()
    DBGU = nc.dram_tensor("s_dbgu", [128, 1024], BF16, kind=skind).ap()
    DBGM = nc.dram_tensor("s_dbgm", [2, 128, 24 + 8 + 1024], F32, kind=skind).ap()

    with contextlib.ExitStack() as st:
        P = Prog(nc, st)
        P.init_arena(196 * 1024)
        pX = P.ps("pX", [128, 2048], F32)
        pY = P.ps("pY", [128, 1024], F32)
        pZ = P.ps("pZ", [128, 1024], F32)
        pYb = pY[:].bitcast(BF16)
        pZb = pZ[:].bitcast(BF16)
        ident = P.sb("ident", [128, 128], F32)
        identb = P.sb("identb", [128, 128], BF16)
        onesf = P.sb("onesf", [128, 128], F32)
        cnst = P.sb("cnst", [128, 64], F32)
        cosT = P.sb("cosT", [128, NT, 16], F32)
        sinT = P.sb("sinT", [128, NT, 16], F32)
        cT = P.sb("cTs", [128, 8], F32)
        scT = P.sb("scT", [128, 8], F32)
        modT = [P.sb(f"modT{l}", [128, 24], F32) for l in range(2)]
        sc1p = [P.sb(f"sc1p{l}", [128, 8], F32) for l in range(2)]
        gate_bc = [P.sb(f"gatebc{l}", [128, D], F32) for l in range(2)]
        wst = [P.sb(f"wst{i}", [128, 3072], F32) for i in range(2)]
        xt = [P.sb(f"xt{i}", [128, D], F32) for i in range(2)]
        stG = [P.sb(f"stG{i}", [128, WA], BF16) for i in range(2)]
        small = P.sb("small", [128, 64], F32)
        M0 = P.mark()
        G = {}

        cast_rr = [0]

        def cast(out_ap, in_ap, reads, writes):
            cast_rr[0] ^= 1
            eng = "dve" if cast_rr[0] else "pool"
            P.op(eng, lambda e: e.tensor_copy(out_ap, in_ap), reads, writes)

        P.dma(ident[:], ident_in, "c0", writes=["ident"])
        P.dma(cosT[:], cos_in, "c1", writes=["cosT"])
        P.dma(sinT[:], sin_in, "c2", writes=["sinT"])
        P.dma(cT[:], cT_in, "c3", writes=["cT"])
        P.op("dve", lambda e: e.tensor_copy(identb[:], ident[:]), ["ident"], ["identb"])
        P.op("pool", lambda e: e.memset(onesf[:], 1.0), [], ["onesf"])
        P.op("pool", lambda e: e.memset(cnst[:, 0:32], -1.0), [], ["cnst"])
        P.op("pool", lambda e: e.memset(cnst[:, 32:64], -0.5), [], ["cnst"])
        P.act(scT[:], cT[:], AF.Silu, ["cT"], ["scT"])

        import os
        modrow = P.sb("modrow", [1, 3 * D], F32)
        adab_row = P.sb("adab_row", [1, 3 * D], F32)
        mbanks = [(pX[0:1, n * 512:(n + 1) * 512], ("pX", n)) for n in range(4)] + \
                 [(pY[0:1, n * 512:(n + 1) * 512], ("pY", n)) for n in range(2)]
        for l in range(2):
            P.dma(adab_row[:], adab_in[l:l + 1, :], "c4", writes=["adab_row"])
            for dc in range(8):
                sl = dc % 2
                P.dma(wst[sl][:], adaw_in[l, dc * 128:(dc + 1) * 128, :], f"wst{sl}", writes=[("wst", sl)])
                for n in range(6):
                    P.mm(mbanks[n][0], scT[:, dc:dc + 1], wst[sl][:, n * 512:(n + 1) * 512], dc == 0, dc == 7,
                         reads=[("wst", sl), "scT"], writes=[mbanks[n][1]], inc=(n == 5))
            P.op("dve", lambda e: e.tensor_tensor(modrow[:, 0:2048], pX[0:1, 0:2048], adab_row[:, 0:2048], ALU.add),
                 [("pX", 0), ("pX", 1), ("pX", 2), ("pX", 3), "adab_row"], ["modrow"])
            P.op("dve", lambda e: e.tensor_tensor(modrow[:, 2048:3072], pY[0:1, 0:1024], adab_row[:, 2048:3072], ALU.add),
                 [("pY", 0), ("pY", 1), "adab_row"], ["modrow"])
            for j in range(16):
                P.mm(pZ[:, j:j + 1], modrow[0:1, j * 128:(j + 1) * 128], onesf[0:1, 0:1], True, True,
                     reads=["modrow", "onesf"], writes=[("pZ", 0)], inc=(j == 15))
            P.op("dve", lambda e, l=l: e.tensor_copy(modT[l][:, 0:16], pZ[:, 0:16]), [("pZ", 0)], [("modT", l)])
            P.op("dve", lambda e, l=l: e.tensor_scalar(sc1p[l][:], modT[l][:, 8:16], 1.0, None, ALU.add),
                 [("modT", l)], [("sc1p", l)])
            for n in range(2):
                P.mm(pX[:, n * 512:(n + 1) * 512], onesf[0:1, 0:128], modrow[0:1, 2048 + n * 512: 2048 + (n + 1) * 512], True, True,
                     reads=["modrow", "onesf"], writes=[("pX", n)])
                P.op("dve", lambda e, l=l, n=n: e.tensor_copy(gate_bc[l][:, n * 512:(n + 1) * 512], pX[:, n * 512:(n + 1) * 512]),
                     [("pX", n)], [("gatebc", l)])

        if debug:
            for l in range(2):
                P.dma(DBGM[l, :, 0:24], modT[l][:], "dbg", reads=[("modT", l)])
                P.dma(DBGM[l, :, 24:32], scT[:], "dbg", reads=["scT"])
                P.dma(DBGM[l, :, 32:1056], gate_bc[l][:], "dbg", reads=[("gatebc", l)])

        def load_weight_bf16(dst, dst_key, src2d, rows, cols, row_chunks, mul=None, mul_key=None):
            cw = 3072
            for rc in range(row_chunks):
                for c0 in range(0, cols, cw):
                    c1 = min(cols, c0 + cw)
                    sl = cast_rr[0]
                    P.dma(wst[sl][:, 0:c1 - c0], src2d[rc * 128:(rc + 1) * 128, c0:c1], f"wst{sl}",
                          writes=[("wst", sl)])
                    if mul is None:
                        cast(dst[:, rc * cols + c0: rc * cols + c1], wst[sl][:, 0:c1 - c0], [("wst", sl)], [dst_key])
                    else:
                        cast_rr[0] ^= 1
                        P.op("dve" if cast_rr[0] else "pool",
                             lambda e, rc=rc, c0=c0, c1=c1, sl=sl: e.tensor_tensor(dst[:, rc * cols + c0: rc * cols + c1],
                                                                                 wst[sl][:, 0:c1 - c0], mul[:, c0:c1], ALU.mult),
                             [("wst", sl), mul_key], [dst_key])

        def make_uT(l, src, tt, sl):
            uT = G["uT"]
            P.dma(xt[sl][:], src[tt * 128:(tt + 1) * 128, :], f"xt{sl}", reads=[("xsrc", l, tt)], writes=[("xt", sl)])
            for dc in range(8):
                P.tr(pY[:, dc * 128:(dc + 1) * 128], xt[sl][:, dc * 128:(dc + 1) * 128], ident[:],
                     reads=[("xt", sl), "ident"], writes=[("pY", dc // 4)], inc=(dc % 4 == 3))
            for dc in range(8):
                o = uT[sl][:, dc * 128:(dc + 1) * 128]
                i_ = pY[:, dc * 128:(dc + 1) * 128]
                if dc // 4 == 0:
                    P.act(o, i_, AF.Identity, [("pY", dc // 4), ("sc1p", l), ("modT", l)], [("uT", sl)],
                          scale=sc1p[l][:, dc:dc + 1], bias=modT[l][:, dc:dc + 1])
                else:
                    P.op("dve", lambda e, o=o, i_=i_, dc=dc: e.tensor_scalar(o, i_, sc1p[l][:, dc:dc + 1], modT[l][:, dc:dc + 1],
                                                                        ALU.mult, ALU.add),
                         [("pY", dc // 4), ("sc1p", l), ("modT", l)], [("uT", sl)])

        pslots = [("pX", 0), ("pX", 1), ("pX", 2), ("pX", 3)]

        def pslot_ap(k, w=512):
            return pX[:, k * 512: k * 512 + w]

        if "A0" in phases:
            wbig = P.sb("wbig", [128, 8 * 4608], BF16)
            uT = G["uT"] = [P.sb(f"uT{i}", [128, 8 * 128], BF16) for i in range(2)]
            stA = [P.sb(f"stA{i}", [128, 3 * WA], BF16) for i in range(2)]
            KL = int(os.environ.get("KLVL", "9"))
            if KL >= 2:
                load_weight_bf16(wbig, "wbig", awin_in, D, 4 * WA, 8)
            def a0_h2(tt, sl):
                    if debug and tt == 0:
                        P.dma(DBGU, uT[sl][:], "dbg", reads=[("uT", sl)])
                    for n in range(int(os.environ.get("KN", "9")) if KL >= 4 else 0):
                        k = n % 4
                        for dc in range(8):
                            P.mm(pslot_ap(k), uT[sl][:, dc * 128:(dc + 1) * 128],
                                 wbig[:, dc * 4608 + n * 512: dc * 4608 + (n + 1) * 512], dc == 0, dc == 7,
                                 reads=[("uT", sl), "wbig"], writes=[pslots[k]])
                        if n < 6:
                            o = stA[sl][:, n * 512:(n + 1) * 512]
                            if n % 2 == 0:
                                P.act(o, pslot_ap(k), AF.Identity, [pslots[k]], [("stA", sl)])
                            else:
                                P.op("dve", lambda e, o=o, k=k: e.tensor_copy(o, pslot_ap(k)), [pslots[k]], [("stA", sl)])
                        elif n == 6:
                            P.act(stA[sl][:, 3072:3456], pslot_ap(k, 384), AF.Identity, [pslots[k]], [("stA", sl)])
                            P.act(stG[sl][:, 0:128], pX[:, k * 512 + 384: k * 512 + 512], AF.Silu, [pslots[k]], [("stG", sl)])
                        else:
                            P.act(stG[sl][:, 128 + (n - 7) * 512: 128 + (n - 6) * 512], pslot_ap(k), AF.Silu,
                                  [pslots[k]], [("stG", sl)])
                    if KL < 5:
                        return
                    P.dma(QKV0[tt * 128:(tt + 1) * 128, :], stA[sl][:], f"oA{sl}", reads=[("stA", sl)],
                          writes=[("QKV0", tt)], eng="pool")
                    P.dma(SG0[tt * 128:(tt + 1) * 128, :], stG[sl][:], f"oG{sl}", reads=[("stG", sl)],
                          writes=[("SG0", tt)], eng="pool")


            P.rec_start(); make_uT(0, x_in, 0, 0); P.emit(P.rec_stop())
            for tt in range(NT):
                h1 = []
                if tt + 1 < NT:
                    P.rec_start(); make_uT(0, x_in, tt + 1, (tt + 1) % 2); h1 = P.rec_stop()
                P.rec_start(); a0_h2(tt, tt % 2); h2 = P.rec_stop()
                P.emit_zip(h1, h2)
            P.release(M0)

        if "B0" in phases:
            bt = P.sb("bt", [128, 36 * 128], BF16)
            maskT = P.sb("maskT", [128, 2 * 128], F32)
            P.dma(maskT[:], mask_in.rearrange("p a b -> p (a b)"), "c5", writes=["maskT"])
            for pc in range(4):
                P.dma(wst[pc % 2][:, 0:1152], btab_in.rearrange("p a b -> p (a b)")[:, pc * 1152:(pc + 1) * 1152],
                      f"wst{pc % 2}", writes=[("wst", pc % 2)])
                for q in range(9):
                    a = pc * 9 + q
                    ty = (a % 12) // 6
                    P.op("dve", lambda e, pc=pc, q=q, a=a, ty=ty: e.scalar_tensor_tensor(
                        bt[:, a * 128:(a + 1) * 128], wst[pc % 2][:, q * 128:(q + 1) * 128], 8.0,
                        maskT[:, ty * 128:(ty + 1) * 128], ALU.mult, ALU.add),
                        [("wst", pc % 2), "maskT"], ["bt"])
            NKS = 4
            kt_sb = [P.sb(f"ktile{i}", [128, 384], BF16) for i in range(NKS)]
            va_sb = [P.sb(f"vaug{i}", [128, 6 * 65], BF16) for i in range(NKS)]
            kT_sb = [P.sb(f"kT{i}", [128, 384], BF16) for i in range(NKS)]
            vst_sb = [P.sb(f"vst{i}", [128, 384], BF16) for i in range(NKS)]
            q_sb = [P.sb(f"qtile{i}", [128, 384], BF16) for i in range(2)]
            qT_sb = [P.sb(f"qT{i}", [128, 384], BF16) for i in range(2)]
            pT_sb = [P.sb(f"pT{i}", [128, 768], BF16) for i in range(2)]
            ao_sb = [P.sb(f"ao{i}", [128, 390], F32) for i in range(2)]
            qall = [("QKV0", t) for t in range(NT)]
            kcount = [0]
            slots = {}
            blocks = []
            for g, (_, dil) in enumerate(GROUPS):
                for r in range(dil):
                    for m in range(S // dil // 128):
                        blocks.append((g, r, m))

            def load_ktile(g, r, n):
                dil = GROUPS[g][1]
                L = S // dil
                Qv = QKV0.rearrange("(s d) c -> d s c", d=dil)
                ks = kcount[0] % NKS
                kcount[0] += 1
                lo = max(0, 128 * n - 64)
                hi = min(L, 128 * n + 64)
                p0 = lo - (128 * n - 64)
                p1 = p0 + (hi - lo)
                va3 = va_sb[ks][:].rearrange("p (h c) -> p h c", c=65)
                edge = (p1 - p0) < 128
                if edge:
                    P.op("pool", lambda e: e.memset(kt_sb[ks][:], 0.0), [], [("ktile", ks)])
                    P.op("pool", lambda e: e.memset(va_sb[ks][:], 0.0), [], [("vaug", ks)])
                    P.op("pool", lambda e: e.memset(va3[p0:p1, :, 64:65], 1.0), [], [("vaug", ks)])
                else:
                    P.op("pool", lambda e: e.memset(va3[:, :, 64:65], 1.0), [], [("vaug", ks)])
                P.dma(kt_sb[ks][p0:p1, :], Qv[r, lo:hi, WA + g * 384: WA + (g + 1) * 384], f"kt{ks}",
                      reads=qall, writes=[("ktile", ks)])
                P.dma(vst_sb[ks][p0:p1, :], Qv[r, lo:hi, 2 * WA + g * 384: 2 * WA + (g + 1) * 384],
                      f"va{ks}", reads=qall, writes=[("vst", ks)])
                P.op("pool", lambda e: e.tensor_copy(va3[p0:p1, :, 0:64], vst_sb[ks][p0:p1, :].rearrange("p (h c) -> p h c", c=64)),
                     [("vst", ks)], [("vaug", ks)])
                for c in range(3):
                    P.tr(pYb[:, c * 128:(c + 1) * 128], kt_sb[ks][:, c * 128:(c + 1) * 128], identb[:],
                         reads=[("ktile", ks), "identb"], writes=[("pY", 0)], inc=(c == 2))
                P.op("dve", lambda e: e.tensor_copy(kT_sb[ks][:], pYb[:, 0:384]), [("pY", 0)], [("kT", ks)])
                slots[(g, r, n)] = ks

            def prep(bi):
                g, r, m = blocks[bi]
                dil = GROUPS[g][1]
                Qv = QKV0.rearrange("(s d) c -> d s c", d=dil)
                if m == 0:
                    load_ktile(g, r, 0)
                load_ktile(g, r, m + 1)
                qs = bi % 2
                P.dma(q_sb[qs][:], Qv[r, 128 * m:128 * (m + 1), g * 384:(g + 1) * 384], f"qt{qs}",
                      reads=qall, writes=[("qtile", qs)])
                for c in range(3):
                    P.tr(pYb[:, 1024 + c * 128: 1024 + (c + 1) * 128], q_sb[qs][:, c * 128:(c + 1) * 128], identb[:],
                         reads=[("qtile", qs), "identb"], writes=[("pY", 1)], inc=(c == 2))
                P.op("dve", lambda e: e.tensor_copy(qT_sb[qs][:], pYb[:, 1024:1408]), [("pY", 1)], [("qT", qs)])

            its = [(bi, hh) for bi in range(len(blocks)) for hh in range(2)]

            def qk0(i):
                bi, hh = its[i]
                g, r, m = blocks[bi]
                qs, ps = bi % 2, i % 2
                ka, kb_ = slots[(g, r, m)], slots[(g, r, m + 1)]
                for hl in range(3):
                    h = hh * 3 + hl
                    c, hp = h // 2, (h % 2) * 64
                    for ty, ks in ((0, ka), (1, kb_)):
                        col = ps * 1024 + (hl * 2 + ty) * 128
                        bank = col // 512
                        P.mm(pX[:, col:col + 128], kT_sb[ks][hp:hp + 64, c * 128:(c + 1) * 128],
                             qT_sb[qs][hp:hp + 64, c * 128:(c + 1) * 128], True, False,
                             reads=[("kT", ks), ("qT", qs)], writes=[("pX", bank)])
                        a = g * 12 + ty * 6 + h
                        P.mm(pX[:, col:col + 128], identb[:], bt[:, a * 128:(a + 1) * 128], False, True,
                             reads=["identb", "bt"], writes=[("pX", bank)],
                             inc=(hl == 2 and ty == 1) or ((col % 512) == 384))

            def ex0(i):
                ps = i % 2
                P.act(pT_sb[ps][:], pX[:, ps * 1024: ps * 1024 + 768], AF.Exp, [("pX", ps * 2), ("pX", ps * 2 + 1)],
                      [("pT", ps)], scale=0.125)

            def pv0(i):
                bi, hh = its[i]
                g, r, m = blocks[bi]
                dil = GROUPS[g][1]
                Av = ATT0.rearrange("(s d) c -> d s c", d=dil)
                ps, zs, ab = i % 2, i % 2, bi % 2
                ka, kb_ = slots[(g, r, m)], slots[(g, r, m + 1)]
                for hl in range(3):
                    h = hh * 3 + hl
                    for ty, ks in ((0, ka), (1, kb_)):
                        col = (hl * 2 + ty) * 128
                        P.mm(pZ[:, zs * 512 + hl * 65: zs * 512 + (hl + 1) * 65], pT_sb[ps][:, col:col + 128],
                             va_sb[ks][:, h * 65:(h + 1) * 65], ty == 0, ty == 1,
                             reads=[("pT", ps), ("vaug", ks)], writes=[("pZ", zs)], inc=(hl == 2 and ty == 1))
                P.op("dve", lambda e: e.tensor_copy(ao_sb[ab][:, hh * 195:(hh + 1) * 195], pZ[:, zs * 512: zs * 512 + 195]),
                     [("pZ", zs)], [("ao", ab)])
                if hh == 1:
                    P.dma(Av[r, 128 * m:128 * (m + 1), g * 390:(g + 1) * 390], ao_sb[ab][:], f"oB{ab}",
                          reads=[("ao", ab)], writes=[("ATT0", g, r, m)], eng="pool")

            prep(0)
            qk0(0)
            for i in range(len(its)):
                bi, hh = its[i]
                if hh == 0 and bi + 1 < len(blocks):
                    prep(bi + 1)
                if i + 1 < len(its):
                    qk0(i + 1)
                ex0(i)
                pv0(i)

            P.release(M0)

        def phase_c(l, H, Wd, ATT, SG, xsrc, dst, w_src, att_keys, sg_keys, x_keys):
            nch = Wd // 128
            wout = P.sb("wout", [128, 9 * D], BF16)
            lng_bc = P.sb("lng_bc", [128, D], F32)
            lnb_bc = P.sb("lnb_bc", [128, D], F32)
            attb = [P.sb(f"attb{i}", [128, 18 * 65], F32) for i in range(2)]
            ytmp = [P.sb(f"ytmp{i}", [128, WA], F32) for i in range(2)]
            ybf = [P.sb(f"ybf{i}", [128, WA], BF16) for i in range(2)]
            yT = [P.sb(f"yT{i}", [128, 9 * 128], BF16) for i in range(2)]
            zt = [P.sb(f"zt{i}", [128, D], F32) for i in range(2)]
            smc = [P.sb(f"smc{i}", [128, 64], F32) for i in range(2)]
            junk = P.sb("junk", [128, D], BF16)
            load_weight_bf16(wout, "wout", w_src, Wd, D, nch, mul=gate_bc[l], mul_key=("gatebc", l))
            P.dma(lng_bc[:], lng_in[l].partition_broadcast(128), "c6", writes=["lng"])
            P.dma(lnb_bc[:], lnb_in[l].partition_broadcast(128), "c7", writes=["lnb"])

            def c_h1(tt, sl):
                sm = smc[sl]
                smk = ("smc", sl)
                P.dma(attb[sl][:, 0:H * 65], ATT[tt * 128:(tt + 1) * 128, :], f"att{sl}", reads=att_keys, writes=[("attb", sl)])
                P.dma(stG[sl][:, 0:Wd], SG[tt * 128:(tt + 1) * 128, :], f"sgl{sl}", reads=sg_keys, writes=[("stG", sl)])
                a3 = attb[sl][:, 0:H * 65].rearrange("p (h c) -> p h c", c=65)
                nd = 6 if l == 0 else 16
                if l == 0:
                    P.op("pool", lambda e: e.tensor_tensor(sm[:, 0:6].unsqueeze(2), a3[:, 0:6, 64:65], a3[:, 6:12, 64:65], ALU.add),
                         [("attb", sl)], [smk])
                    P.op("pool", lambda e: e.tensor_tensor(sm[:, 0:6].unsqueeze(2), sm[:, 0:6].unsqueeze(2), a3[:, 12:18, 64:65], ALU.add),
                         [("attb", sl), smk], [smk])
                else:
                    P.op("pool", lambda e: e.tensor_copy(sm[:, 0:16].unsqueeze(2), a3[:, 0:16, 64:65]),
                         [("attb", sl)], [smk])
                P.op("pool", lambda e: e.tensor_tensor(sm[:, 16:16 + nd], sm[:, 0:nd], cnst[:, 0:nd], ALU.pow),
                     [smk, "cnst"], [smk])

            def c_s1b(tt, sl):
                sm = smc[sl]
                smk = ("smc", sl)
                a3 = attb[sl][:, 0:H * 65].rearrange("p (h c) -> p h c", c=65)
                y3 = ytmp[sl][:, 0:Wd].rearrange("p (h c) -> p h c", c=64)
                if l == 0:
                    for g in range(3):
                        P.op("dve", lambda e, g=g: e.tensor_tensor(
                            y3[:, g * 6:(g + 1) * 6, :], a3[:, g * 6:(g + 1) * 6, 0:64],
                            sm[:, 16:22].unsqueeze(2).to_broadcast([128, 6, 64]), ALU.mult),
                            [("attb", sl), smk], [("ytmp", sl)])
                else:
                    P.op("dve", lambda e: e.tensor_tensor(
                        y3, a3[:, :, 0:64], sm[:, 16:32].unsqueeze(2).to_broadcast([128, 16, 64]), ALU.mult),
                        [("attb", sl), smk], [("ytmp", sl)])
                P.op("dve", lambda e: e.tensor_tensor(ybf[sl][:, 0:Wd], ytmp[sl][:, 0:Wd], stG[sl][:, 0:Wd], ALU.mult),
                     [("ytmp", sl), ("stG", sl)], [("ybf", sl)])

            def c_s2(tt, sl):
                sm = smc[sl]
                smk = ("smc", sl)
                P.dma(xt[sl][:], xsrc[tt * 128:(tt + 1) * 128, :], f"xt{sl}", reads=x_keys, writes=[("xt", sl)])
                for c in range(nch):
                    P.tr(pYb[:, c * 128:(c + 1) * 128], ybf[sl][:, c * 128:(c + 1) * 128], identb[:],
                         reads=[("ybf", sl), "identb"], writes=[("pY", 0), ("pY", 1)], inc=(c == nch - 1))
                P.op("dve", lambda e: e.tensor_copy(yT[sl][:, 0:nch * 128], pYb[:, 0:nch * 128]), [("pY", 0), ("pY", 1)], [("yT", sl)])

            def c_s2b(tt, sl):
                sm = smc[sl]
                smk = ("smc", sl)
                for n in range(2):
                    for c in range(nch):
                        P.mm(pZ[:, n * 512:(n + 1) * 512], yT[sl][:, c * 128:(c + 1) * 128],
                             wout[:, c * D + n * 512: c * D + (n + 1) * 512], c == 0, c == nch - 1,
                             reads=[("yT", sl), "wout"], writes=[("pZ", n)])
                z = zt[sl]
                P.op("dve", lambda e: e.scalar_tensor_tensor(z[:], xt[sl][:], ALPHA, pZ[:], ALU.mult, ALU.add, accum_out=sm[:, 32:33]),
                     [("pZ", 0), ("pZ", 1), ("xt", sl)], [("zt", sl), smk])

            def c_h2(tt, sl):
                sm = smc[sl]
                smk = ("smc", sl)
                z = zt[sl]
                P.op("pool", lambda e: e.tensor_scalar(sm[:, 33:34], sm[:, 32:33], -1.0 / D, None, ALU.mult), [smk], [smk])
                P.act(junk[:], z[:], AF.Square, [("zt", sl), smk], ["junk", smk], bias=sm[:, 33:34], scale=1.0, accum_out=sm[:, 34:35])
                P.op("pool", lambda e: e.tensor_scalar(sm[:, 35:36], sm[:, 34:35], 1.0 / D, 1e-5, ALU.mult, ALU.add), [smk], [smk])
                P.op("pool", lambda e: e.tensor_tensor(sm[:, 36:37], sm[:, 35:36], cnst[:, 32:33], ALU.pow), [smk, "cnst"], [smk])
                P.op("pool", lambda e: e.tensor_tensor(sm[:, 37:38], sm[:, 33:34], sm[:, 36:37], ALU.mult), [smk], [smk])
                P.act(z[:], z[:], AF.Identity, [("zt", sl), smk], [("zt", sl)], scale=sm[:, 36:37], bias=sm[:, 37:38])
                P.op("dve", lambda e: e.tensor_tensor(z[:], z[:], lng_bc[:], ALU.mult), [("zt", sl), "lng"], [("zt", sl)])
                P.op("pool", lambda e: e.tensor_tensor(z[:], z[:], lnb_bc[:], ALU.add), [("zt", sl), "lnb"], [("zt", sl)])
                P.dma(dst[tt * 128:(tt + 1) * 128, :], z[:], f"oC{sl}", reads=[("zt", sl)], writes=[("xsrc", l + 1, tt)], eng="pool")

            def rec(f, t):
                if t < 0 or t >= NT:
                    return []
                P.rec_start(); f(t, t % 2); return P.rec_stop()

            for step in range(NT + 4):
                P.emit_zipn([rec(c_h1, step), rec(c_s1b, step - 1), rec(c_s2, step - 2), rec(c_s2b, step - 3), rec(c_h2, step - 4)])
            P.release(M0)

        if "C0" in phases:
            att_keys = [("ATT0", g, r, m) for g, (_, dil) in enumerate(GROUPS) for r in range(dil) for m in range(S // dil // 128)]
            phase_c(0, 18, WA, ATT0, SG0, x_in, X1 if nph > 3 or debug else out, awout_in, att_keys,
                    [("SG0", t) for t in range(NT)], [])

        if "A1" in phases:
            W1 = 1440
            wbig = P.sb("wbig1", [128, 8 * W1], BF16)
            uT = G["uT"] = [P.sb(f"uTb{i}", [128, 8 * 128], BF16) for i in range(2)]
            stA = [P.sb(f"stAb{i}", [128, 3072], BF16) for i in range(2)]
            stV = [P.sb(f"stV{i}", [128, WB], BF16) for i in range(2)]
            load_weight_bf16(wbig, "wbig", bwin_in, D, W1, 8)
            wuq = P.sb("wuq", [128, 2 * 1536], BF16)
            wukv = P.sb("wukv", [128, 2048], BF16)
            load_weight_bf16(wuq, "wuq", bwuq_in, 256, 1536, 2)
            load_weight_bf16(wukv, "wukv", bwukv_in, 128, 2048, 1)
            qn_bc = P.sb("qn_bc", [128, 384], F32)
            P.dma(qn_bc[:, 0:256], bqn_in.partition_broadcast(128), "c8", writes=["qn_bc"])
            P.dma(qn_bc[:, 256:384], bkvn_in.partition_broadcast(128), "c9", writes=["qn_bc"])
            cn_ = [P.sb(f"cn{i}", [128, 384], BF16) for i in range(2)]
            cnb_ = [P.sb(f"cnb{i}", [128, 384], BF16) for i in range(2)]
            cnT_ = [P.sb(f"cnT{i}", [128, 384], BF16) for i in range(2)]
            kr_ = [P.sb(f"kr{i}", [128, 64], F32) for i in range(2)]
            krb_ = [P.sb(f"krb{i}", [128, 32], BF16) for i in range(2)]
            rt_ = [P.sb(f"rt{i}", [128, 16 * 32], F32) for i in range(2)]
            sm1_ = [P.sb(f"sm1{i}", [128, 64], F32) for i in range(2)]

            ring = [(pX[:, 1024:1536], ("pX", 2)), (pX[:, 1536:2048], ("pX", 3)), (pZ[:, 512:1024], ("pZ", 1))]
            rcnt = [0]

            def nxt():
                r = ring[rcnt[0] % len(ring)]
                rcnt[0] += 1
                return r

            def a1_h2(tt, sl):
                cn, cnb, cnT, kr, krb, rt, small = cn_[sl], cnb_[sl], cnT_[sl], kr_[sl], krb_[sl], rt_[sl], sm1_[sl]
                smk = ("small1", sl)
                bl, bk = pX[:, 0:512], ("pX", 0)
                for dc in range(8):
                    P.mm(bl[:, 0:416], uT[sl][:, dc * 128:(dc + 1) * 128], wbig[:, dc * W1: dc * W1 + 416], dc == 0, dc == 7,
                         reads=[("uT", sl), "wbig"], writes=[bk])
                for gc in range(2):
                    bg, bgk = pX[:, 512:1024], ("pX", 1)
                    for dc in range(8):
                        P.mm(bg, uT[sl][:, dc * 128:(dc + 1) * 128],
                             wbig[:, dc * W1 + 416 + gc * 512: dc * W1 + 416 + (gc + 1) * 512], dc == 0, dc == 7,
                             reads=[("uT", sl), "wbig"], writes=[bgk])
                    P.act(stG[sl][:, gc * 512:(gc + 1) * 512], bg, AF.Silu, [bgk], [("stG", sl)])
                P.dma(SG1[tt * 128:(tt + 1) * 128, :], stG[sl][:, 0:WB], f"oG{sl}", reads=[("stG", sl)],
                      writes=[("SG1", tt)], eng="pool")
                P.act(cn[:, 0:256], bl[:, 0:256], AF.Square, [bk], [("cn1", sl), smk], accum_out=small[:, 40:41])
                P.act(cn[:, 256:384], bl[:, 256:384], AF.Square, [bk], [("cn1", sl), smk], accum_out=small[:, 41:42])
                P.op("pool", lambda e: e.tensor_scalar(small[:, 42:43], small[:, 40:41], 1.0 / 256, 1e-6, ALU.mult, ALU.add), [smk], [smk])
                P.op("pool", lambda e: e.tensor_scalar(small[:, 43:44], small[:, 41:42], 1.0 / 128, 1e-6, ALU.mult, ALU.add), [smk], [smk])
                P.op("pool", lambda e: e.tensor_tensor(small[:, 44:46], small[:, 42:44], cnst[:, 32:34], ALU.pow), [smk, "cnst"], [smk])
                P.op("dve", lambda e: e.scalar_tensor_tensor(cnb[:, 0:256], bl[:, 0:256], small[:, 44:45], qn_bc[:, 0:256], ALU.mult, ALU.mult),
                     [bk, smk, "qn_bc"], [("cnb1", sl)])
                P.op("dve", lambda e: e.scalar_tensor_tensor(cnb[:, 256:384], bl[:, 256:384], small[:, 45:46], qn_bc[:, 256:384], ALU.mult, ALU.mult),
                     [bk, smk, "qn_bc"], [("cnb1", sl)])
                P.op("dve", lambda e: e.tensor_copy(kr[:, 0:32], bl[:, 384:416]), [bk], [("kr1", sl)])

            def a1_t2(tt, sl):
                cn, cnb, cnT, kr, krb, rt, small = cn_[sl], cnb_[sl], cnT_[sl], kr_[sl], krb_[sl], rt_[sl], sm1_[sl]
                for c in range(3):
                    P.tr(pZb[:, c * 128: (c + 1) * 128], cnb[:, c * 128:(c + 1) * 128], identb[:],
                         reads=[("cnb1", sl), "identb"], writes=[("pZ", 0)], inc=(c == 2))
                P.op("dve", lambda e: e.tensor_copy(cnT[:], pZb[:, 0:384]), [("pZ", 0)], [("cnT1", sl)])
                cs = cosT[:, tt, :]
                sn = sinT[:, tt, :]
                krk = ("kr1", sl)
                P.op("pool", lambda e: e.tensor_tensor(kr[:, 32:48], kr[:, 0:16], cs, ALU.mult), [krk, "cosT"], [krk])
                P.op("pool", lambda e: e.tensor_tensor(kr[:, 48:64], kr[:, 16:32], sn, ALU.mult), [krk, "sinT"], [krk])
                P.op("pool", lambda e: e.tensor_tensor(krb[:, 0:16], kr[:, 32:48], kr[:, 48:64], ALU.subtract), [krk], [("krb1", sl)])
                P.op("pool", lambda e: e.tensor_tensor(kr[:, 32:48], kr[:, 16:32], cs, ALU.mult), [krk, "cosT", ("krb1", sl)], [krk])
                P.op("pool", lambda e: e.tensor_tensor(kr[:, 48:64], kr[:, 0:16], sn, ALU.mult), [krk, "sinT"], [krk])
                P.op("pool", lambda e: e.tensor_tensor(krb[:, 16:32], kr[:, 32:48], kr[:, 48:64], ALU.add), [krk], [("krb1", sl)])

            def a1_s2b(tt, sl):
                cn, cnb, cnT, kr, krb, rt, small = cn_[sl], cnb_[sl], cnT_[sl], kr_[sl], krb_[sl], rt_[sl], sm1_[sl]
                csb = cosT[:, tt, :].unsqueeze(1).to_broadcast([128, 4, 16])
                snb = sinT[:, tt, :].unsqueeze(1).to_broadcast([128, 4, 16])
                for qc in range(4):
                    bq, bqk = nxt()
                    for c in range(2):
                        P.mm(bq[:, 0:384], cnT[:, c * 128:(c + 1) * 128],
                             wuq[:, c * 1536 + qc * 384: c * 1536 + (qc + 1) * 384], c == 0, c == 1,
                             reads=[("cnT1", sl), "wuq"], writes=[bqk])
                    q3 = bq[:, 0:384].rearrange("p (h c) -> p h c", c=96)
                    o3 = stA[sl][:, qc * 384:(qc + 1) * 384].rearrange("p (h c) -> p h c", c=96)
                    r3 = rt[:, qc * 128:(qc + 1) * 128].rearrange("p (h c) -> p h c", c=32)
                    rk = ("rt1", sl, qc)
                    ok = ("stA", sl)
                    P.act(o3[:, :, 0:64], q3[:, :, 0:64], AF.Identity, [bqk], [ok])
                    P.op("dve", lambda e, q3=q3, r3=r3: e.tensor_tensor(r3[:, :, 0:16], q3[:, :, 64:80], csb, ALU.mult), [bqk, "cosT"], [rk])
                    P.op("dve", lambda e, q3=q3, r3=r3: e.tensor_tensor(r3[:, :, 16:32], q3[:, :, 80:96], snb, ALU.mult), [bqk, "sinT"], [rk])
                    P.op("pool", lambda e, o3=o3, r3=r3: e.tensor_tensor(o3[:, :, 64:80], r3[:, :, 0:16], r3[:, :, 16:32], ALU.subtract), [rk], [ok])
                    P.op("dve", lambda e, q3=q3, r3=r3: e.tensor_tensor(r3[:, :, 0:16], q3[:, :, 80:96], csb, ALU.mult), [bqk, "cosT"], [rk])
                    P.op("dve", lambda e, q3=q3, r3=r3: e.tensor_tensor(r3[:, :, 16:32], q3[:, :, 64:80], snb, ALU.mult), [bqk, "sinT"], [rk])
                    P.op("pool", lambda e, o3=o3, r3=r3: e.tensor_tensor(o3[:, :, 80:96], r3[:, :, 0:16], r3[:, :, 16:32], ALU.add), [rk], [ok])
                P.dma(Q1[tt * 128:(tt + 1) * 128, :], stA[sl][:, 0:1536], f"oA{sl}", reads=[("stA", sl)], writes=[("Q1", tt)], eng="pool")
                for kc in range(4):
                    bv, bvk = nxt()
                    P.mm(bv, cnT[:, 256:384], wukv[:, kc * 512:(kc + 1) * 512], True, True,
                         reads=[("cnT1", sl), "wukv"], writes=[bvk])
                    kv3 = bv.rearrange("p (h c) -> p h c", c=128)
                    k3 = stA[sl][:, 1536 + kc * 384: 1536 + (kc + 1) * 384].rearrange("p (h c) -> p h c", c=96)
                    v3 = stV[sl][:, kc * 256:(kc + 1) * 256].rearrange("p (h c) -> p h c", c=64)
                    P.act(k3[:, :, 0:64], kv3[:, :, 0:64], AF.Identity, [bvk], [("stK", sl)])
                    P.op("dve", lambda e, kv3=kv3, v3=v3: e.tensor_copy(v3, kv3[:, :, 64:128]), [bvk], [("stV", sl)])
                k3a = stA[sl][:, 1536:3072].rearrange("p (h c) -> p h c", c=96)
                P.op("pool", lambda e: e.tensor_copy(k3a[:, :, 64:96], krb[:].unsqueeze(1).to_broadcast([128, 16, 32])),
                     [("krb1", sl)], [("stK", sl)])
                P.dma(K1[tt * 128:(tt + 1) * 128, :], stA[sl][:, 1536:3072], f"oK{sl}", reads=[("stK", sl)], writes=[("K1", tt)], eng="pool")
                P.dma(V1[tt * 128:(tt + 1) * 128, :], stV[sl][:], f"oV{sl}", reads=[("stV", sl)], writes=[("V1", tt)], eng="pool")

            def rec1(f, t):
                if t < 0 or t >= NT:
                    return []
                P.rec_start(); f(t, t % 2); return P.rec_stop()

            for step in range(NT + 3):
                P.emit_zipn([rec1(lambda t, s: make_uT(1, X1, t, s), step), rec1(a1_h2, step - 1), rec1(a1_t2, step - 2),
                             rec1(a1_s2b, step - 3)])

            P.release(M0)

        if "B1" in phases:
            qh = P.sb("qh", [128, NT * 96], BF16)
            kh = P.sb("kh", [128, NT * 96], BF16)
            vh = [P.sb(f"vh{i}", [128, NT * 65], BF16) for i in range(2)]
            qTh = [P.sb(f"qTh{i}", [96, S], BF16) for i in range(2)]
            kTh = [P.sb(f"kTh{i}", [96, S], BF16) for i in range(2)]
            pT1 = [P.sb(f"pT1_{i}", [128, 1024], BF16) for i in range(2)]
            ost = [P.sb(f"ost{i}", [128, NT * 65], F32) for i in range(2)]
            q1k = [("Q1", t) for t in range(NT)]
            k1k = [("K1", t) for t in range(NT)]
            v1k = [("V1", t) for t in range(NT)]
            Q1v = Q1.rearrange("(t p) (h c) -> p t h c", p=128, c=96)
            K1v = K1.rearrange("(t p) (h c) -> p t h c", p=128, c=96)
            V1v = V1.rearrange("(t p) (h c) -> p t h c", p=128, c=64)
            A1v = ATT1.rearrange("(t p) (h c) -> p t h c", p=128, c=65)
            sc1 = float(96 ** -0.5)
            for i in range(2):
                P.op("pool", lambda e, i=i: e.memset(vh[i][:], 1.0), [], [("vh", i)])
            def prep(h):
                hs = h % 2
                P.dma(qh[:].rearrange("p (t c) -> p t c", c=96), Q1v[:, :, h, :], "qh", reads=q1k, writes=["qh"])
                P.dma(kh[:].rearrange("p (t c) -> p t c", c=96), K1v[:, :, h, :], "kh", reads=k1k, writes=["kh"])
                P.dma(vh[hs][:].rearrange("p (t c) -> p t c", c=65)[:, :, 0:64], V1v[:, :, h, :], f"vh{hs}", reads=v1k, writes=[("vh", hs)])
                for src, dstT, key in ((qh, qTh[hs], ("qTh", hs)), (kh, kTh[hs], ("kTh", hs))):
                    for t8 in range(4):
                        bank = t8 % 2
                        for j in range(8):
                            t = t8 * 8 + j
                            P.tr(pYb[0:96, bank * 1024 + j * 128: bank * 1024 + (j + 1) * 128], src[:, t * 96:(t + 1) * 96], identb[:],
                                 reads=["qh" if src is qh else "kh", "identb"], writes=[("pY", bank)], inc=(j == 7))
                        P.op("dve", lambda e, dstT=dstT, t8=t8, bank=bank: e.tensor_copy(
                            dstT[:, t8 * 1024:(t8 + 1) * 1024], pYb[0:96, bank * 1024:(bank + 1) * 1024]),
                            [("pY", bank)], [key])

            its = [(h, qg, kp) for h in range(16) for qg in range(8) for kp in range(16)]

            def qk(i):
                h, qg, kp = its[i]
                hs, ps = h % 2, i % 2
                for j in range(2):
                    kt = kp * 2 + j
                    P.mm(pX[:, ps * 1024 + j * 512: ps * 1024 + (j + 1) * 512], kTh[hs][:, kt * 128:(kt + 1) * 128],
                         qTh[hs][:, qg * 512:(qg + 1) * 512], True, True,
                         reads=[("kTh", hs), ("qTh", hs)], writes=[("pX", ps * 2 + j)], inc=True)

            def ex(i):
                ps = i % 2
                P.act(pT1[ps][:], pX[:, ps * 1024:(ps + 1) * 1024], AF.Exp, [("pX", ps * 2), ("pX", ps * 2 + 1)],
                      [("pT1", ps)], scale=sc1)

            def pv(i):
                h, qg, kp = its[i]
                hs, ps, zs = h % 2, i % 2, qg % 2
                for j in range(2):
                    kt = kp * 2 + j
                    for qs in range(4):
                        P.mm(pZ[:, zs * 512 + qs * 65: zs * 512 + (qs + 1) * 65],
                             pT1[ps][:, j * 512 + qs * 128: j * 512 + (qs + 1) * 128],
                             vh[hs][:, kt * 65:(kt + 1) * 65], (kt == 0 and qs == 0), kt == 31,
                             reads=[("pT1", ps), ("vh", hs)], writes=[("pZ", zs)], inc=(j == 1 and qs == 3))
                if kp == 15:
                    P.op("dve", lambda e: e.tensor_copy(ost[hs][:, qg * 260:(qg + 1) * 260], pZ[:, zs * 512: zs * 512 + 260]),
                         [("pZ", zs)], [("ost", hs)])
                    if qg == 7:
                        P.dma(A1v[:, :, h, :], ost[hs][:].rearrange("p (t c) -> p t c", c=65), f"oB{hs}", reads=[("ost", hs)],
                              writes=[("ATT1", h)], eng="pool")

            prep(0)
            qk(0)
            for i in range(len(its)):
                h, qg, kp = its[i]
                if qg == 2 and kp == 0 and h + 1 < 16:
                    prep(h + 1)
                if i + 1 < len(its):
                    qk(i + 1)
                ex(i)
                pv(i)
            P.release(M0)

        if "C1" in phases:
            phase_c(1, 16, WB, ATT1, SG1, X1, out, bwout_in, [("ATT1", h) for h in range(16)],
                    [("SG1", t) for t in range(NT)], [("xsrc", 1, t) for t in range(NT)])

        P.finish()
    return nc


def make_in_maps(inputs):
    f = lambda a: np.ascontiguousarray(np.asarray(a, dtype=np.float32))
    x = f(inputs["x"]); c = f(inputs["c"]); rel_bias = f(inputs["rel_bias"])
    mask, bidx, cos, sin = _const_tables()
    btab = np.zeros((128, 36, 128), np.float32)
    for g in range(3):
        for ty in range(2):
            for h in range(6):
                btab[:, g * 12 + ty * 6 + h, :] = rel_bias[bidx[g, ty], g * 6 + h]
    shared = {
        "ada_w": f(inputs["ada_w"]),
        "ada_b": f(inputs["ada_b"]),
        "ln_g": f(inputs["ln_g"]), "ln_b": f(inputs["ln_b"]),
        "a_w_in": f(inputs["a_w_in"])[0], "a_w_out": f(inputs["a_w_out"])[0],
        "b_w_in": f(inputs["b_w_in"])[0], "b_q_norm": f(inputs["b_q_norm"])[0],
        "b_w_uq": f(inputs["b_w_uq"])[0], "b_kv_norm": f(inputs["b_kv_norm"])[0],
        "b_w_ukv": f(inputs["b_w_ukv"])[0], "b_w_out": f(inputs["b_w_out"])[0],
        "ident": np.eye(128, dtype=np.float32),
        "cosT": np.ascontiguousarray(cos.reshape(NT, 128, 16).transpose(1, 0, 2)),
        "sinT": np.ascontiguousarray(sin.reshape(NT, 128, 16).transpose(1, 0, 2)),
        "maskT": np.ascontiguousarray(mask.transpose(1, 0, 2)),
        "btab": btab,
    }
    maps = []
    for b in range(8):
        m = dict(shared)
        m["x"] = np.ascontiguousarray(x[b])
        m["cT"] = np.ascontiguousarray(c[b].reshape(8, 128).T)
        maps.append(m)
    return maps


_NC_CACHE = {}


def kernel(**inputs):
    if "nc" not in _NC_CACHE:
        _NC_CACHE["nc"] = build_program()
    nc = _NC_CACHE["nc"]
    maps = make_in_maps(inputs)
    res = run_bass_kernel_spmd(nc, maps, core_ids=list(range(8)))
    return np.stack([np.asarray(r["out"], dtype=np.float32) for r in res.results], axis=0)
```

```python
import contextlib
import numpy as np
import concourse.bass as bass
import concourse.mybir as mybir
from concourse.bass_utils import run_bass_kernel_spmd

F32 = mybir.dt.float32
BF16 = mybir.dt.bfloat16
AF = mybir.ActivationFunctionType
ALU = mybir.AluOpType
AX = mybir.AxisListType


class Prog:
    ENG = ("pe", "act", "dve", "pool", "sp")

    def __init__(self, nc, stack):
        self.nc = nc
        self.stack = stack
        self.stream = {e: [] for e in self.ENG}
        self.semh = {}
        self.semcnt = {}
        self.waited = {e: {} for e in self.ENG}
        self.res = {}
        self.nbuf = 0
        self.self_wait = True
        self.barrier_req = {e: {} for e in self.ENG}

    def init_arena(self, nbytes):
        self.stack.enter_context(self.nc.sbuf_tensor("arena", [128, nbytes // 2], BF16))
        self.a0 = int(self.nc.sbuf_base) - nbytes
        self.aoff = 0
        self.acap = nbytes

    def sb(self, name, shape, dt):
        n = 1
        for s in shape[1:]:
            n *= int(s)
        nb = n * (4 if dt == F32 else 2)
        nb = (nb + 63) // 64 * 64
        off = self.aoff
        self.aoff += nb
        assert self.aoff <= self.acap, ("SBUF arena overflow", name, self.aoff, self.acap)
        return self.nc.alloc_sbuf_tensor_at(name, list(shape), dt, offset=self.a0 + off)

    def mark(self):
        return self.aoff

    def release(self, m):
        for e in self.ENG:
            req = self.barrier_req[e]
            for s, c in self.semcnt.items():
                if c > 0 and req.get(s, 0) < c:
                    req[s] = c
        self.aoff = m

    def ps(self, name, shape, dt=F32):
        return self.stack.enter_context(self.nc.psum_tensor(name, list(shape), dt))

    def dram(self, name, shape, dt, kind="Internal"):
        return self.nc.dram_tensor(name, list(shape), dt, kind=kind).ap()

    def _sem(self, name):
        if name not in self.semh:
            self.semh[name] = self.stack.enter_context(self.nc.semaphore(name))
            self.semcnt[name] = 0
        return name

    def rec_start(self):
        self._rec = []

    def rec_stop(self):
        r = self._rec
        self._rec = None
        return r

    def emit(self, lst):
        for a in lst:
            self.op(*a)

    def emit_zip(self, a, b):
        i = j = 0
        na, nb = len(a), len(b)
        while i < na or j < nb:
            if j >= nb or (i < na and i * nb <= j * na):
                self.op(*a[i])
                i += 1
            else:
                self.op(*b[j])
                j += 1

    def emit_zipn(self, lists):
        lists = [l for l in lists if l]
        idx = [0] * len(lists)
        while True:
            best, bf_ = -1, 2.0
            for k, l in enumerate(lists):
                if idx[k] < len(l):
                    f = idx[k] / len(l)
                    if f < bf_:
                        best, bf_ = k, f
            if best < 0:
                break
            self.op(*lists[best][idx[best]])
            idx[best] += 1

    def op(self, eng, fn, reads=(), writes=(), inc=True, chan=None):
        if getattr(self, "_rec", None) is not None:
            self._rec.append((eng, fn, tuple(reads), tuple(writes), inc, chan))
            return None
        is_dma = chan is not None
        deps = {}
        writes = list(writes) + [r for r in reads if isinstance(r, tuple) and r[0] in ("pX", "pY", "pZ")]

        def need(t):
            if t is None:
                return
            s, v = t
            if s == "c_" + eng and (eng == "pe" or not self.self_wait):
                return
            if deps.get(s, 0) < v:
                deps[s] = v

        for r in reads:
            st = self.res.get(r)
            if st is not None:
                for t in st["w"].items():
                    need(t)
        for w in writes:
            st = self.res.get(w)
            if st is not None:
                for t in st["w"].items():
                    need(t)
                for t in st["r"].items():
                    need(t)
        if self.barrier_req[eng]:
            for s, v in self.barrier_req[eng].items():
                if s != "c_" + eng and deps.get(s, 0) < v:
                    deps[s] = v
            self.barrier_req[eng] = {}
        waits = []
        wd = self.waited[eng]
        for s, v in deps.items():
            if wd.get(s, 0) < v:
                wd[s] = v
                waits.append((s, v))
        if is_dma:
            s = self._sem("d_" + chan)
            self.semcnt[s] += 16
            tick = (s, self.semcnt[s])
            incinfo = (s, 16)
        else:
            s = self._sem("c_" + eng)
            if inc:
                self.semcnt[s] += 1
                tick = (s, self.semcnt[s])
                incinfo = (s, 1)
            else:
                tick = (s, self.semcnt[s] + 1)
                incinfo = None
        self.stream[eng].append((waits, fn, incinfo))
        for r in reads:
            st = self.res.setdefault(r, {"w": {}, "r": {}})
            if st["r"].get(tick[0], 0) < tick[1]:
                st["r"][tick[0]] = tick[1]
        for w in writes:
            st = self.res.setdefault(w, {"w": {}, "r": {}})
            if st["w"].get(tick[0], 0) < tick[1]:
                st["w"][tick[0]] = tick[1]
        return tick

    def mm(self, out, lhsT, rhs, start, stop, reads=(), writes=(), inc=None, **kw):
        if inc is None:
            inc = bool(stop)
        return self.op("pe", lambda e: e.matmul(out, lhsT, rhs, start=start, stop=stop, **kw),
                       reads, writes, inc=inc)

    def tr(self, out, in_, ident, reads=(), writes=(), inc=True):
        return self.op("pe", lambda e: e.transpose(out, in_, ident), reads, writes, inc=inc)

    def act(self, out, in_, func, reads=(), writes=(), eng="act", **kw):
        return self.op(eng, lambda e: e.activation(out, in_, func, **kw), reads, writes)

    def dma(self, out, in_, chan, reads=(), writes=(), eng="sp", **kw):
        return self.op(eng, lambda e: e.dma_start(out=out, in_=in_, **kw), reads, writes, chan=chan)

    def finish(self, final_waits=()):
        fw = {}
        for r in final_waits:
            st = self.res.get(r)
            if st and st["w"]:
                for s, v in st["w"].items():
                    fw[s] = max(fw.get(s, 0), v)
        for s, c in self.semcnt.items():
            if s.startswith("d_") and c > 0:
                fw[s] = max(fw.get(s, 0), c)
        nc = self.nc
        semh = self.semh
        streams = self.stream

        def run(e, key):
            for waits, fn, incinfo in streams[key]:
                for s, v in waits:
                    e.wait_ge(semh[s], v)
                ins = fn(e)
                if incinfo is not None:
                    ins.then_inc(semh[incinfo[0]], incinfo[1])

        with nc.Block() as block:
            @block.tensor
            def _(e):
                run(e, "pe")

            @block.scalar
            def _(e):
                run(e, "act")

            @block.vector
            def _(e):
                run(e, "dve")

            @block.gpsimd
            def _(e):
                run(e, "pool")

            @block.sync
            def _(e):
                run(e, "sp")
                for s, v in fw.items():
                    e.wait_ge(semh[s], v)


import math

S = 4096
D = 1024
NT = 32
WA = 1152
WB = 1024
ALPHA = float((2.0 * 2) ** 0.25)
GROUPS = ((128, 1), (512, 4), (2048, 16))


def _t5_bucket_np(rel):
    half, max_exact = 16, 8
    base = np.where(rel > 0, half, 0)
    n = np.abs(rel)
    nf = np.maximum(n, 1).astype(np.float32)
    large = max_exact + (np.log(nf / np.float32(max_exact)) / np.float32(math.log(1024 / max_exact))
                         * np.float32(half - max_exact)).astype(np.int32)
    large = np.minimum(large, half - 1)
    return base + np.where(n < max_exact, n, large)


def _const_tables():
    i = np.arange(128)[:, None]
    j = np.arange(128)[None, :]
    offA = i - 64 - j
    offB = 64 + i - j
    mask = np.zeros((2, 128, 128), np.float32)
    mask[0][~(i >= j)] = -240000.0
    mask[1][~(i <= j)] = -240000.0
    bidx = np.zeros((3, 2, 128, 128), np.int64)
    for g, (_, dil) in enumerate(GROUPS):
        bidx[g, 0] = _t5_bucket_np(np.clip(offA, -64, 64) * dil)
        bidx[g, 1] = _t5_bucket_np(np.clip(offB, -64, 64) * dil)
    pos = np.arange(S, dtype=np.float32)
    inv_freq = (np.float32(10000.0) ** (-np.arange(0, 32, 2, dtype=np.float32) / np.float32(32))).astype(np.float32)
    ang = (pos[:, None] * inv_freq[None, :]).astype(np.float32)
    cos = np.cos(ang).astype(np.float32)
    sin = np.sin(ang).astype(np.float32)
    return mask, bidx, cos, sin


def build_program(debug=False, stop_after="C1"):
    nc = bass.Bass("TRN2", target_bir_lowering=False)
    order = ["A0", "B0", "C0", "A1", "B1", "C1"]
    nph = order.index(stop_after) + 1
    phases = set(order[:nph])

    def din(name, shape, dt=F32):
        return nc.dram_tensor(name, list(shape), dt, kind="ExternalInput").ap()

    x_in = din("x", [S, D])
    cT_in = din("cT", [128, 8])
    adaw_in = din("ada_w", [2, D, 3 * D])
    adab_in = din("ada_b", [2, 3 * D])
    lng_in = din("ln_g", [2, D])
    lnb_in = din("ln_b", [2, D])
    awin_in = din("a_w_in", [D, 4 * WA])
    awout_in = din("a_w_out", [WA, D])
    bwin_in = din("b_w_in", [D, 1440])
    bqn_in = din("b_q_norm", [256])
    bwuq_in = din("b_w_uq", [256, 1536])
    bkvn_in = din("b_kv_norm", [128])
    bwukv_in = din("b_w_ukv", [128, 2048])
    bwout_in = din("b_w_out", [WB, D])
    ident_in = din("ident", [128, 128])
    cos_in = din("cosT", [128, NT, 16])
    sin_in = din("sinT", [128, NT, 16])
    mask_in = din("maskT", [128, 2, 128])
    btab_in = din("btab", [128, 36, 128])

    okind = "ExternalOutput"
    skind = "ExternalOutput" if debug else "Internal"
    out = nc.dram_tensor("out", [S, D], F32, kind=okind).ap()
    QKV0 = nc.dram_tensor("s_qkv0", [S, 3 * WA], BF16, kind=skind).ap()
    SG0 = nc.dram_tensor("s_sg0", [S, WA], BF16, kind=skind).ap()
    ATT0 = nc.dram_tensor("s_att0", [S, 18 * 65], F32, kind=skind).ap()
    X1 = nc.dram_tensor("s_x1", [S, D], F32, kind=skind).ap()
    Q1 = nc.dram_tensor("s_q1", [S, 16 * 96], BF16, kind=skind).ap()
    K1 = nc.dram_tensor("s_k1", [S, 16 * 96], BF16, kind=skind).ap()
    V1 = nc.dram_tensor("s_v1", [S, 16 * 64], BF16, kind=skind).ap()
    SG1 = nc.dram_tensor("s_sg1", [S, WB], BF16, kind=skind).ap()
    ATT1 = nc.dram_tensor("s_att1", [S, 16 * 65], F32, kind=skind).ap()
    DBGU = nc.dram_tensor("s_dbgu", [128, 1024], BF16, kind=skind).ap()
    DBGM = nc.dram_tensor("s_dbgm", [2, 128, 24 + 8 + 1024], F32, kind=skind).ap()

    with contextlib.ExitStack() as st:
        P = Prog(nc, st)
        P.init_arena(196 * 1024)
        pX = P.ps("pX", [128, 2048], F32)
        pY = P.ps("pY", [128, 1024], F32)
        pZ = P.ps("pZ", [128, 1024], F32)
        pYb = pY[:].bitcast(BF16)
        pZb = pZ[:].bitcast(BF16)
        ident = P.sb("ident", [128, 128], F32)
        identb = P.sb("identb", [128, 128], BF16)
        onesf = P.sb("onesf", [128, 128], F32)
        cnst = P.sb("cnst", [128, 64], F32)
        cosT = P.sb("cosT", [128, NT, 16], F32)
        sinT = P.sb("sinT", [128, NT, 16], F32)
        cT = P.sb("cTs", [128, 8], F32)
        scT = P.sb("scT", [128, 8], F32)
        modT = [P.sb(f"modT{l}", [128, 24], F32) for l in range(2)]
        sc1p = [P.sb(f"sc1p{l}", [128, 8], F32) for l in range(2)]
        gate_bc = [P.sb(f"gatebc{l}", [128, D], F32) for l in range(2)]
        wst = [P.sb(f"wst{i}", [128, 3072], F32) for i in range(2)]
        xt = [P.sb(f"xt{i}", [128, D], F32) for i in range(2)]
        stG = [P.sb(f"stG{i}", [128, WA], BF16) for i in range(2)]
        small = P.sb("small", [128, 64], F32)
        M0 = P.mark()
        G = {}

        cast_rr = [0]

        def cast(out_ap, in_ap, reads, writes):
            cast_rr[0] ^= 1
            eng = "dve" if cast_rr[0] else "pool"
            P.op(eng, lambda e: e.tensor_copy(out_ap, in_ap), reads, writes)

        P.dma(ident[:], ident_in, "c0", writes=["ident"])
        P.dma(cosT[:], cos_in, "c1", writes=["cosT"])
        P.dma(sinT[:], sin_in, "c2", writes=["sinT"])
        P.dma(cT[:], cT_in, "c3", writes=["cT"])
        P.op("dve", lambda e: e.tensor_copy(identb[:], ident[:]), ["ident"], ["identb"])
        P.op("pool", lambda e: e.memset(onesf[:], 1.0), [], ["onesf"])
        P.op("pool", lambda e: e.memset(cnst[:, 0:32], -1.0), [], ["cnst"])
        P.op("pool", lambda e: e.memset(cnst[:, 32:64], -0.5), [], ["cnst"])
        P.act(scT[:], cT[:], AF.Silu, ["cT"], ["scT"])

        import os
        modrow = P.sb("modrow", [1, 3 * D], F32)
        adab_row = P.sb("adab_row", [1, 3 * D], F32)
        mbanks = [(pX[0:1, n * 512:(n + 1) * 512], ("pX", n)) for n in range(4)] + \
                 [(pY[0:1, n * 512:(n + 1) * 512], ("pY", n)) for n in range(2)]
        for l in range(2):
            P.dma(adab_row[:], adab_in[l:l + 1, :], "c4", writes=["adab_row"])
            for dc in range(8):
                sl = dc % 2
                P.dma(wst[sl][:], adaw_in[l, dc * 128:(dc + 1) * 128, :], f"wst{sl}", writes=[("wst", sl)])
                for n in range(6):
                    P.mm(mbanks[n][0], scT[:, dc:dc + 1], wst[sl][:, n * 512:(n + 1) * 512], dc == 0, dc == 7,
                         reads=[("wst", sl), "scT"], writes=[mbanks[n][1]], inc=(n == 5))
            P.op("dve", lambda e: e.tensor_tensor(modrow[:, 0:2048], pX[0:1, 0:2048], adab_row[:, 0:2048], ALU.add),
                 [("pX", 0), ("pX", 1), ("pX", 2), ("pX", 3), "adab_row"], ["modrow"])
            P.op("dve", lambda e: e.tensor_tensor(modrow[:, 2048:3072], pY[0:1, 0:1024], adab_row[:, 2048:3072], ALU.add),
                 [("pY", 0), ("pY", 1), "adab_row"], ["modrow"])
            for j in range(16):
                P.mm(pZ[:, j:j + 1], modrow[0:1, j * 128:(j + 1) * 128], onesf[0:1, 0:1], True, True,
                     reads=["modrow", "onesf"], writes=[("pZ", 0)], inc=(j == 15))
            P.op("dve", lambda e, l=l: e.tensor_copy(modT[l][:, 0:16], pZ[:, 0:16]), [("pZ", 0)], [("modT", l)])
            P.op("dve", lambda e, l=l: e.tensor_scalar(sc1p[l][:], modT[l][:, 8:16], 1.0, None, ALU.add),
                 [("modT", l)], [("sc1p", l)])
            for n in range(2):
                P.mm(pX[:, n * 512:(n + 1) * 512], onesf[0:1, 0:128], modrow[0:1, 2048 + n * 512: 2048 + (n + 1) * 512], True, True,
                     reads=["modrow", "onesf"], writes=[("pX", n)])
                P.op("dve", lambda e, l=l, n=n: e.tensor_copy(gate_bc[l][:, n * 512:(n + 1) * 512], pX[:, n * 512:(n + 1) * 512]),
                     [("pX", n)], [("gatebc", l)])

        if debug:
            for l in range(2):
                P.dma(DBGM[l, :, 0:24], modT[l][:], "dbg", reads=[("modT", l)])
                P.dma(DBGM[l, :, 24:32], scT[:], "dbg", reads=["scT"])
                P.dma(DBGM[l, :, 32:1056], gate_bc[l][:], "dbg", reads=[("gatebc", l)])

        def load_weight_bf16(dst, dst_key, src2d, rows, cols, row_chunks, mul=None, mul_key=None):
            cw = 3072
            for rc in range(row_chunks):
                for c0 in range(0, cols, cw):
                    c1 = min(cols, c0 + cw)
                    sl = cast_rr[0]
                    P.dma(wst[sl][:, 0:c1 - c0], src2d[rc * 128:(rc + 1) * 128, c0:c1], f"wst{sl}",
                          writes=[("wst", sl)])
                    if mul is None:
                        cast(dst[:, rc * cols + c0: rc * cols + c1], wst[sl][:, 0:c1 - c0], [("wst", sl)], [dst_key])
                    else:
                        cast_rr[0] ^= 1
                        P.op("dve" if cast_rr[0] else "pool",
                             lambda e, rc=rc, c0=c0, c1=c1, sl=sl: e.tensor_tensor(dst[:, rc * cols + c0: rc * cols + c1],
                                                                                 wst[sl][:, 0:c1 - c0], mul[:, c0:c1], ALU.mult),
                             [("wst", sl), mul_key], [dst_key])

        def make_uT(l, src, tt, sl):
            uT = G["uT"]
            P.dma(xt[sl][:], src[tt * 128:(tt + 1) * 128, :], f"xt{sl}", reads=[("xsrc", l, tt)], writes=[("xt", sl)])
            for dc in range(8):
                P.tr(pY[:, dc * 128:(dc + 1) * 128], xt[sl][:, dc * 128:(dc + 1) * 128], ident[:],
                     reads=[("xt", sl), "ident"], writes=[("pY", dc // 4)], inc=(dc % 4 == 3))
            for dc in range(8):
                o = uT[sl][:, dc * 128:(dc + 1) * 128]
                i_ = pY[:, dc * 128:(dc + 1) * 128]
                if dc // 4 == 0:
                    P.act(o, i_, AF.Identity, [("pY", dc // 4), ("sc1p", l), ("modT", l)], [("uT", sl)],
                          scale=sc1p[l][:, dc:dc + 1], bias=modT[l][:, dc:dc + 1])
                else:
                    P.op("dve", lambda e, o=o, i_=i_, dc=dc: e.tensor_scalar(o, i_, sc1p[l][:, dc:dc + 1], modT[l][:, dc:dc + 1],
                                                                        ALU.mult, ALU.add),
                         [("pY", dc // 4), ("sc1p", l), ("modT", l)], [("uT", sl)])

        pslots = [("pX", 0), ("pX", 1), ("pX", 2), ("pX", 3)]

        def pslot_ap(k, w=512):
            return pX[:, k * 512: k * 512 + w]

        if "A0" in phases:
            wbig = P.sb("wbig", [128, 8 * 4608], BF16)
            uT = G["uT"] = [P.sb(f"uT{i}", [128, 8 * 128], BF16) for i in range(2)]
            stA = [P.sb(f"stA{i}", [128, 3 * WA], BF16) for i in range(2)]
            KL = int(os.environ.get("KLVL", "9"))
            if KL >= 2:
                load_weight_bf16(wbig, "wbig", awin_in, D, 4 * WA, 8)
            def a0_h2(tt, sl):
                    if debug and tt == 0:
                        P.dma(DBGU, uT[sl][:], "dbg", reads=[("uT", sl)])
                    for n in range(int(os.environ.get("KN", "9")) if KL >= 4 else 0):
                        k = n % 4
                        for dc in range(8):
                            P.mm(pslot_ap(k), uT[sl][:, dc * 128:(dc + 1) * 128],
                                 wbig[:, dc * 4608 + n * 512: dc * 4608 + (n + 1) * 512], dc == 0, dc == 7,
                                 reads=[("uT", sl), "wbig"], writes=[pslots[k]])
                        if n < 6:
                            o = stA[sl][:, n * 512:(n + 1) * 512]
                            if n % 2 == 0:
                                P.act(o, pslot_ap(k), AF.Identity, [pslots[k]], [("stA", sl)])
                            else:
                                P.op("dve", lambda e, o=o, k=k: e.tensor_copy(o, pslot_ap(k)), [pslots[k]], [("stA", sl)])
                        elif n == 6:
                            P.act(stA[sl][:, 3072:3456], pslot_ap(k, 384), AF.Identity, [pslots[k]], [("stA", sl)])
                            P.act(stG[sl][:, 0:128], pX[:, k * 512 + 384: k * 512 + 512], AF.Silu, [pslots[k]], [("stG", sl)])
                        else:
                            P.act(stG[sl][:, 128 + (n - 7) * 512: 128 + (n - 6) * 512], pslot_ap(k), AF.Silu,
                                  [pslots[k]], [("stG", sl)])
                    if KL < 5:
                        return
                    P.dma(QKV0[tt * 128:(tt + 1) * 128, :], stA[sl][:], f"oA{sl}", reads=[("stA", sl)],
                          writes=[("QKV0", tt)], eng="pool")
                    P.dma(SG0[tt * 128:(tt + 1) * 128, :], stG[sl][:], f"oG{sl}", reads=[("stG", sl)],
                          writes=[("SG0", tt)], eng="pool")


            P.rec_start(); make_uT(0, x_in, 0, 0); P.emit(P.rec_stop())
            for tt in range(NT):
                h1 = []
                if tt + 1 < NT:
                    P.rec_start(); make_uT(0, x_in, tt + 1, (tt + 1) % 2); h1 = P.rec_stop()
                P.rec_start(); a0_h2(tt, tt % 2); h2 = P.rec_stop()
                P.emit_zip(h1, h2)
            P.release(M0)

        if "B0" in phases:
            bt = P.sb("bt", [128, 36 * 128], BF16)
            maskT = P.sb("maskT", [128, 2 * 128], F32)
            P.dma(maskT[:], mask_in.rearrange("p a b -> p (a b)"), "c5", writes=["maskT"])
            for pc in range(4):
                P.dma(wst[pc % 2][:, 0:1152], btab_in.rearrange("p a b -> p (a b)")[:, pc * 1152:(pc + 1) * 1152],
                      f"wst{pc % 2}", writes=[("wst", pc % 2)])
                for q in range(9):
                    a = pc * 9 + q
                    ty = (a % 12) // 6
                    P.op("dve", lambda e, pc=pc, q=q, a=a, ty=ty: e.scalar_tensor_tensor(
                        bt[:, a * 128:(a + 1) * 128], wst[pc % 2][:, q * 128:(q + 1) * 128], 8.0,
                        maskT[:, ty * 128:(ty + 1) * 128], ALU.mult, ALU.add),
                        [("wst", pc % 2), "maskT"], ["bt"])
            NKS = 4
            kt_sb = [P.sb(f"ktile{i}", [128, 384], BF16) for i in range(NKS)]
            va_sb = [P.sb(f"vaug{i}", [128, 6 * 65], BF16) for i in range(NKS)]
            kT_sb = [P.sb(f"kT{i}", [128, 384], BF16) for i in range(NKS)]
            vst_sb = [P.sb(f"vst{i}", [128, 384], BF16) for i in range(NKS)]
            q_sb = [P.sb(f"qtile{i}", [128, 384], BF16) for i in range(2)]
            qT_sb = [P.sb(f"qT{i}", [128, 384], BF16) for i in range(2)]
            pT_sb = [P.sb(f"pT{i}", [128, 768], BF16) for i in range(2)]
            ao_sb = [P.sb(f"ao{i}", [128, 390], F32) for i in range(2)]
            qall = [("QKV0", t) for t in range(NT)]
            kcount = [0]
            slots = {}
            blocks = []
            for g, (_, dil) in enumerate(GROUPS):
                for r in range(dil):
                    for m in range(S // dil // 128):
                        blocks.append((g, r, m))

            def load_ktile(g, r, n):
                dil = GROUPS[g][1]
                L = S // dil
                Qv = QKV0.rearrange("(s d) c -> d s c", d=dil)
                ks = kcount[0] % NKS
                kcount[0] += 1
                lo = max(0, 128 * n - 64)
                hi = min(L, 128 * n + 64)
                p0 = lo - (128 * n - 64)
                p1 = p0 + (hi - lo)
                va3 = va_sb[ks][:].rearrange("p (h c) -> p h c", c=65)
                edge = (p1 - p0) < 128
                if edge:
                    P.op("pool", lambda e: e.memset(kt_sb[ks][:], 0.0), [], [("ktile", ks)])
                    P.op("pool", lambda e: e.memset(va_sb[ks][:], 0.0), [], [("vaug", ks)])
                    P.op("pool", lambda e: e.memset(va3[p0:p1, :, 64:65], 1.0), [], [("vaug", ks)])
                else:
                    P.op("pool", lambda e: e.memset(va3[:, :, 64:65], 1.0), [], [("vaug", ks)])
                P.dma(kt_sb[ks][p0:p1, :], Qv[r, lo:hi, WA + g * 384: WA + (g + 1) * 384], f"kt{ks}",
                      reads=qall, writes=[("ktile", ks)])
                P.dma(vst_sb[ks][p0:p1, :], Qv[r, lo:hi, 2 * WA + g * 384: 2 * WA + (g + 1) * 384],
                      f"va{ks}", reads=qall, writes=[("vst", ks)])
                P.op("pool", lambda e: e.tensor_copy(va3[p0:p1, :, 0:64], vst_sb[ks][p0:p1, :].rearrange("p (h c) -> p h c", c=64)),
                     [("vst", ks)], [("vaug", ks)])
                for c in range(3):
                    P.tr(pYb[:, c * 128:(c + 1) * 128], kt_sb[ks][:, c * 128:(c + 1) * 128], identb[:],
                         reads=[("ktile", ks), "identb"], writes=[("pY", 0)], inc=(c == 2))
                P.op("dve", lambda e: e.tensor_copy(kT_sb[ks][:], pYb[:, 0:384]), [("pY", 0)], [("kT", ks)])
                slots[(g, r, n)] = ks

            def prep(bi):
                g, r, m = blocks[bi]
                dil = GROUPS[g][1]
                Qv = QKV0.rearrange("(s d) c -> d s c", d=dil)
                if m == 0:
                    load_ktile(g, r, 0)
                load_ktile(g, r, m + 1)
                qs = bi % 2
                P.dma(q_sb[qs][:], Qv[r, 128 * m:128 * (m + 1), g * 384:(g + 1) * 384], f"qt{qs}",
                      reads=qall, writes=[("qtile", qs)])
                for c in range(3):
                    P.tr(pYb[:, 1024 + c * 128: 1024 + (c + 1) * 128], q_sb[qs][:, c * 128:(c + 1) * 128], identb[:],
                         reads=[("qtile", qs), "identb"], writes=[("pY", 1)], inc=(c == 2))
                P.op("dve", lambda e: e.tensor_copy(qT_sb[qs][:], pYb[:, 1024:1408]), [("pY", 1)], [("qT", qs)])

            its = [(bi, hh) for bi in range(len(blocks)) for hh in range(2)]

            def qk0(i):
                bi, hh = its[i]
                g, r, m = blocks[bi]
                qs, ps = bi % 2, i % 2
                ka, kb_ = slots[(g, r, m)], slots[(g, r, m + 1)]
                for hl in range(3):
                    h = hh * 3 + hl
                    c, hp = h // 2, (h % 2) * 64
                    for ty, ks in ((0, ka), (1, kb_)):
                        col = ps * 1024 + (hl * 2 + ty) * 128
                        bank = col // 512
                        P.mm(pX[:, col:col + 128], kT_sb[ks][hp:hp + 64, c * 128:(c + 1) * 128],
                             qT_sb[qs][hp:hp + 64, c * 128:(c + 1) * 128], True, False,
                             reads=[("kT", ks), ("qT", qs)], writes=[("pX", bank)])
                        a = g * 12 + ty * 6 + h
                        P.mm(pX[:, col:col + 128], identb[:], bt[:, a * 128:(a + 1) * 128], False, True,
                             reads=["identb", "bt"], writes=[("pX", bank)],
                             inc=(hl == 2 and ty == 1) or ((col % 512) == 384))

            def ex0(i):
                ps = i % 2
                P.act(pT_sb[ps][:], pX[:, ps * 1024: ps * 1024 + 768], AF.Exp, [("pX", ps * 2), ("pX", ps * 2 + 1)],
                      [("pT", ps)], scale=0.125)

            def pv0(i):
                bi, hh = its[i]
                g, r, m = blocks[bi]
                dil = GROUPS[g][1]
                Av = ATT0.rearrange("(s d) c -> d s c", d=dil)
                ps, zs, ab = i % 2, i % 2, bi % 2
                ka, kb_ = slots[(g, r, m)], slots[(g, r, m + 1)]
                for hl in range(3):
                    h = hh * 3 + hl
                    for ty, ks in ((0, ka), (1, kb_)):
                        col = (hl * 2 + ty) * 128
                        P.mm(pZ[:, zs * 512 + hl * 65: zs * 512 + (hl + 1) * 65], pT_sb[ps][:, col:col + 128],
                             va_sb[ks][:, h * 65:(h + 1) * 65], ty == 0, ty == 1,
                             reads=[("pT", ps), ("vaug", ks)], writes=[("pZ", zs)], inc=(hl == 2 and ty == 1))
                P.op("dve", lambda e: e.tensor_copy(ao_sb[ab][:, hh * 195:(hh + 1) * 195], pZ[:, zs * 512: zs * 512 + 195]),
                     [("pZ", zs)], [("ao", ab)])
                if hh == 1:
                    P.dma(Av[r, 128 * m:128 * (m + 1), g * 390:(g + 1) * 390], ao_sb[ab][:], f"oB{ab}",
                          reads=[("ao", ab)], writes=[("ATT0", g, r, m)], eng="pool")

            prep(0)
            qk0(0)
            for i in range(len(its)):
                bi, hh = its[i]
                if hh == 0 and bi + 1 < len(blocks):
                    prep(bi + 1)
                if i + 1 < len(its):
                    qk0(i + 1)
                ex0(i)
                pv0(i)

            P.release(M0)

        def phase_c(l, H, Wd, ATT, SG, xsrc, dst, w_src, att_keys, sg_keys, x_keys):
            nch = Wd // 128
            wout = P.sb("wout", [128, 9 * D], BF16)
            lng_bc = P.sb("lng_bc", [128, D], F32)
            lnb_bc = P.sb("lnb_bc", [128, D], F32)
            attb = [P.sb(f"attb{i}", [128, 18 * 65], F32) for i in range(2)]
            ytmp = [P.sb(f"ytmp{i}", [128, WA], F32) for i in range(2)]
            ybf = [P.sb(f"ybf{i}", [128, WA], BF16) for i in range(2)]
            yT = [P.sb(f"yT{i}", [128, 9 * 128], BF16) for i in range(2)]
            zt = [P.sb(f"zt{i}", [128, D], F32) for i in range(2)]
            smc = [P.sb(f"smc{i}", [128, 64], F32) for i in range(2)]
            junk = P.sb("junk", [128, D], BF16)
            load_weight_bf16(wout, "wout", w_src, Wd, D, nch, mul=gate_bc[l], mul_key=("gatebc", l))
            P.dma(lng_bc[:], lng_in[l].partition_broadcast(128), "c6", writes=["lng"])
            P.dma(lnb_bc[:], lnb_in[l].partition_broadcast(128), "c7", writes=["lnb"])

            def c_h1(tt, sl):
                sm = smc[sl]
                smk = ("smc", sl)
                P.dma(attb[sl][:, 0:H * 65], ATT[tt * 128:(tt + 1) * 128, :], f"att{sl}", reads=att_keys, writes=[("attb", sl)])
                P.dma(stG[sl][:, 0:Wd], SG[tt * 128:(tt + 1) * 128, :], f"sgl{sl}", reads=sg_keys, writes=[("stG", sl)])
                a3 = attb[sl][:, 0:H * 65].rearrange("p (h c) -> p h c", c=65)
                nd = 6 if l == 0 else 16
                if l == 0:
                    P.op("pool", lambda e: e.tensor_tensor(sm[:, 0:6].unsqueeze(2), a3[:, 0:6, 64:65], a3[:, 6:12, 64:65], ALU.add),
                         [("attb", sl)], [smk])
                    P.op("pool", lambda e: e.tensor_tensor(sm[:, 0:6].unsqueeze(2), sm[:, 0:6].unsqueeze(2), a3[:, 12:18, 64:65], ALU.add),
                         [("attb", sl), smk], [smk])
                else:
                    P.op("pool", lambda e: e.tensor_copy(sm[:, 0:16].unsqueeze(2), a3[:, 0:16, 64:65]),
                         [("attb", sl)], [smk])
                P.op("pool", lambda e: e.tensor_tensor(sm[:, 16:16 + nd], sm[:, 0:nd], cnst[:, 0:nd], ALU.pow),
                     [smk, "cnst"], [smk])

            def c_s1b(tt, sl):
                sm = smc[sl]
                smk = ("smc", sl)
                a3 = attb[sl][:, 0:H * 65].rearrange("p (h c) -> p h c", c=65)
                y3 = ytmp[sl][:, 0:Wd].rearrange("p (h c) -> p h c", c=64)
                if l == 0:
                    for g in range(3):
                        P.op("dve", lambda e, g=g: e.tensor_tensor(
                            y3[:, g * 6:(g + 1) * 6, :], a3[:, g * 6:(g + 1) * 6, 0:64],
                            sm[:, 16:22].unsqueeze(2).to_broadcast([128, 6, 64]), ALU.mult),
                            [("attb", sl), smk], [("ytmp", sl)])
                else:
                    P.op("dve", lambda e: e.tensor_tensor(
                        y3, a3[:, :, 0:64], sm[:, 16:32].unsqueeze(2).to_broadcast([128, 16, 64]), ALU.mult),
                        [("attb", sl), smk], [("ytmp", sl)])
                P.op("dve", lambda e: e.tensor_tensor(ybf[sl][:, 0:Wd], ytmp[sl][:, 0:Wd], stG[sl][:, 0:Wd], ALU.mult),
                     [("ytmp", sl), ("stG", sl)], [("ybf", sl)])

            def c_s2(tt, sl):
                sm = smc[sl]
                smk = ("smc", sl)
                P.dma(xt[sl][:], xsrc[tt * 128:(tt + 1) * 128, :], f"xt{sl}", reads=x_keys, writes=[("xt", sl)])
                for c in range(nch):
                    P.tr(pYb[:, c * 128:(c + 1) * 128], ybf[sl][:, c * 128:(c + 1) * 128], identb[:],
                         reads=[("ybf", sl), "identb"], writes=[("pY", 0), ("pY", 1)], inc=(c == nch - 1))
                P.op("dve", lambda e: e.tensor_copy(yT[sl][:, 0:nch * 128], pYb[:, 0:nch * 128]), [("pY", 0), ("pY", 1)], [("yT", sl)])

            def c_s2b(tt, sl):
                sm = smc[sl]
                smk = ("smc", sl)
                for n in range(2):
                    for c in range(nch):
                        P.mm(pZ[:, n * 512:(n + 1) * 512], yT[sl][:, c * 128:(c + 1) * 128],
                             wout[:, c * D + n * 512: c * D + (n + 1) * 512], c == 0, c == nch - 1,
                             reads=[("yT", sl), "wout"], writes=[("pZ", n)])
                z = zt[sl]
                P.op("dve", lambda e: e.scalar_tensor_tensor(z[:], xt[sl][:], ALPHA, pZ[:], ALU.mult, ALU.add, accum_out=sm[:, 32:33]),
                     [("pZ", 0), ("pZ", 1), ("xt", sl)], [("zt", sl), smk])

            def c_h2(tt, sl):
                sm = smc[sl]
                smk = ("smc", sl)
                z = zt[sl]
                P.op("pool", lambda e: e.tensor_scalar(sm[:, 33:34], sm[:, 32:33], -1.0 / D, None, ALU.mult), [smk], [smk])
                P.act(junk[:], z[:], AF.Square, [("zt", sl), smk], ["junk", smk], bias=sm[:, 33:34], scale=1.0, accum_out=sm[:, 34:35])
                P.op("pool", lambda e: e.tensor_scalar(sm[:, 35:36], sm[:, 34:35], 1.0 / D, 1e-5, ALU.mult, ALU.add), [smk], [smk])
                P.op("pool", lambda e: e.tensor_tensor(sm[:, 36:37], sm[:, 35:36], cnst[:, 32:33], ALU.pow), [smk, "cnst"], [smk])
                P.op("pool", lambda e: e.tensor_tensor(sm[:, 37:38], sm[:, 33:34], sm[:, 36:37], ALU.mult), [smk], [smk])
                P.act(z[:], z[:], AF.Identity, [("zt", sl), smk], [("zt", sl)], scale=sm[:, 36:37], bias=sm[:, 37:38])
                P.op("dve", lambda e: e.tensor_tensor(z[:], z[:], lng_bc[:], ALU.mult), [("zt", sl), "lng"], [("zt", sl)])
                P.op("pool", lambda e: e.tensor_tensor(z[:], z[:], lnb_bc[:], ALU.add), [("zt", sl), "lnb"], [("zt", sl)])
                P.dma(dst[tt * 128:(tt + 1) * 128, :], z[:], f"oC{sl}", reads=[("zt", sl)], writes=[("xsrc", l + 1, tt)], eng="pool")

            def rec(f, t):
                if t < 0 or t >= NT:
                    return []
                P.rec_start(); f(t, t % 2); return P.rec_stop()

            for step in range(NT + 4):
                P.emit_zipn([rec(c_h1, step), rec(c_s1b, step - 1), rec(c_s2, step - 2), rec(c_s2b, step - 3), rec(c_h2, step - 4)])
            P.release(M0)

        if "C0" in phases:
            att_keys = [("ATT0", g, r, m) for g, (_, dil) in enumerate(GROUPS) for r in range(dil) for m in range(S // dil // 128)]
            phase_c(0, 18, WA, ATT0, SG0, x_in, X1 if nph > 3 or debug else out, awout_in, att_keys,
                    [("SG0", t) for t in range(NT)], [])

        if "A1" in phases:
            W1 = 1440
            wbig = P.sb("wbig1", [128, 8 * W1], BF16)
            uT = G["uT"] = [P.sb(f"uTb{i}", [128, 8 * 128], BF16) for i in range(2)]
            stA = [P.sb(f"stAb{i}", [128, 3072], BF16) for i in range(2)]
            stV = [P.sb(f"stV{i}", [128, WB], BF16) for i in range(2)]
            load_weight_bf16(wbig, "wbig", bwin_in, D, W1, 8)
            wuq = P.sb("wuq", [128, 2 * 1536], BF16)
            wukv = P.sb("wukv", [128, 2048], BF16)
            load_weight_bf16(wuq, "wuq", bwuq_in, 256, 1536, 2)
            load_weight_bf16(wukv, "wukv", bwukv_in, 128, 2048, 1)
            qn_bc = P.sb("qn_bc", [128, 384], F32)
            P.dma(qn_bc[:, 0:256], bqn_in.partition_broadcast(128), "c8", writes=["qn_bc"])
            P.dma(qn_bc[:, 256:384], bkvn_in.partition_broadcast(128), "c9", writes=["qn_bc"])
            cn_ = [P.sb(f"cn{i}", [128, 384], BF16) for i in range(2)]
            cnb_ = [P.sb(f"cnb{i}", [128, 384], BF16) for i in range(2)]
            cnT_ = [P.sb(f"cnT{i}", [128, 384], BF16) for i in range(2)]
            kr_ = [P.sb(f"kr{i}", [128, 64], F32) for i in range(2)]
            krb_ = [P.sb(f"krb{i}", [128, 32], BF16) for i in range(2)]
            rt_ = [P.sb(f"rt{i}", [128, 16 * 32], F32) for i in range(2)]
            sm1_ = [P.sb(f"sm1{i}", [128, 64], F32) for i in range(2)]

            ring = [(pX[:, 1024:1536], ("pX", 2)), (pX[:, 1536:2048], ("pX", 3)), (pZ[:, 512:1024], ("pZ", 1))]
            rcnt = [0]

            def nxt():
                r = ring[rcnt[0] % len(ring)]
                rcnt[0] += 1
                return r

            def a1_h2(tt, sl):
                cn, cnb, cnT, kr, krb, rt, small = cn_[sl], cnb_[sl], cnT_[sl], kr_[sl], krb_[sl], rt_[sl], sm1_[sl]
                smk = ("small1", sl)
                bl, bk = pX[:, 0:512], ("pX", 0)
                for dc in range(8):
                    P.mm(bl[:, 0:416], uT[sl][:, dc * 128:(dc + 1) * 128], wbig[:, dc * W1: dc * W1 + 416], dc == 0, dc == 7,
                         reads=[("uT", sl), "wbig"], writes=[bk])
                for gc in range(2):
                    bg, bgk = pX[:, 512:1024], ("pX", 1)
                    for dc in range(8):
                        P.mm(bg, uT[sl][:, dc * 128:(dc + 1) * 128],
                             wbig[:, dc * W1 + 416 + gc * 512: dc * W1 + 416 + (gc + 1) * 512], dc == 0, dc == 7,
                             reads=[("uT", sl), "wbig"], writes=[bgk])
                    P.act(stG[sl][:, gc * 512:(gc + 1) * 512], bg, AF.Silu, [bgk], [("stG", sl)])
                P.dma(SG1[tt * 128:(tt + 1) * 128, :], stG[sl][:, 0:WB], f"oG{sl}", reads=[("stG", sl)],
                      writes=[("SG1", tt)], eng="pool")
                P.act(cn[:, 0:256], bl[:, 0:256], AF.Square, [bk], [("cn1", sl), smk], accum_out=small[:, 40:41])
                P.act(cn[:, 256:384], bl[:, 256:384], AF.Square, [bk], [("cn1", sl), smk], accum_out=small[:, 41:42])
                P.op("pool", lambda e: e.tensor_scalar(small[:, 42:43], small[:, 40:41], 1.0 / 256, 1e-6, ALU.mult, ALU.add), [smk], [smk])
                P.op("pool", lambda e: e.tensor_scalar(small[:, 43:44], small[:, 41:42], 1.0 / 128, 1e-6, ALU.mult, ALU.add), [smk], [smk])
                P.op("pool", lambda e: e.tensor_tensor(small[:, 44:46], small[:, 42:44], cnst[:, 32:34], ALU.pow), [smk, "cnst"], [smk])
                P.op("dve", lambda e: e.scalar_tensor_tensor(cnb[:, 0:256], bl[:, 0:256], small[:, 44:45], qn_bc[:, 0:256], ALU.mult, ALU.mult),
                     [bk, smk, "qn_bc"], [("cnb1", sl)])
                P.op("dve", lambda e: e.scalar_tensor_tensor(cnb[:, 256:384], bl[:, 256:384], small[:, 45:46], qn_bc[:, 256:384], ALU.mult, ALU.mult),
                     [bk, smk, "qn_bc"], [("cnb1", sl)])
                P.op("dve", lambda e: e.tensor_copy(kr[:, 0:32], bl[:, 384:416]), [bk], [("kr1", sl)])
                for c in range(3):
                    P.tr(pZb[:, c * 128: (c + 1) * 128], cnb[:, c * 128:(c + 1) * 128], identb[:],
                         reads=[("cnb1", sl), "identb"], writes=[("pZ", 0)], inc=(c == 2))
                P.op("dve", lambda e: e.tensor_copy(cnT[:], pZb[:, 0:384]), [("pZ", 0)], [("cnT1", sl)])
                cs = cosT[:, tt, :]
                sn = sinT[:, tt, :]
                krk = ("kr1", sl)
                P.op("pool", lambda e: e.tensor_tensor(kr[:, 32:48], kr[:, 0:16], cs, ALU.mult), [krk, "cosT"], [krk])
                P.op("pool", lambda e: e.tensor_tensor(kr[:, 48:64], kr[:, 16:32], sn, ALU.mult), [krk, "sinT"], [krk])
                P.op("pool", lambda e: e.tensor_tensor(krb[:, 0:16], kr[:, 32:48], kr[:, 48:64], ALU.subtract), [krk], [("krb1", sl)])
                P.op("pool", lambda e: e.tensor_tensor(kr[:, 32:48], kr[:, 16:32], cs, ALU.mult), [krk, "cosT", ("krb1", sl)], [krk])
                P.op("pool", lambda e: e.tensor_tensor(kr[:, 48:64], kr[:, 0:16], sn, ALU.mult), [krk, "sinT"], [krk])
                P.op("pool", lambda e: e.tensor_tensor(krb[:, 16:32], kr[:, 32:48], kr[:, 48:64], ALU.add), [krk], [("krb1", sl)])

            def a1_s2b(tt, sl):
                cn, cnb, cnT, kr, krb, rt, small = cn_[sl], cnb_[sl], cnT_[sl], kr_[sl], krb_[sl], rt_[sl], sm1_[sl]
                csb = cosT[:, tt, :].unsqueeze(1).to_broadcast([128, 4, 16])
                snb = sinT[:, tt, :].unsqueeze(1).to_broadcast([128, 4, 16])
                for qc in range(4):
                    bq, bqk = nxt()
                    for c in range(2):
                        P.mm(bq[:, 0:384], cnT[:, c * 128:(c + 1) * 128],
                             wuq[:, c * 1536 + qc * 384: c * 1536 + (qc + 1) * 384], c == 0, c == 1,
                             reads=[("cnT1", sl), "wuq"], writes=[bqk])
                    q3 = bq[:, 0:384].rearrange("p (h c) -> p h c", c=96)
                    o3 = stA[sl][:, qc * 384:(qc + 1) * 384].rearrange("p (h c) -> p h c", c=96)
                    r3 = rt[:, qc * 128:(qc + 1) * 128].rearrange("p (h c) -> p h c", c=32)
                    rk = ("rt1", sl, qc)
                    ok = ("stA", sl)
                    P.act(o3[:, :, 0:64], q3[:, :, 0:64], AF.Identity, [bqk], [ok])
                    P.op("dve", lambda e, q3=q3, r3=r3: e.tensor_tensor(r3[:, :, 0:16], q3[:, :, 64:80], csb, ALU.mult), [bqk, "cosT"], [rk])
                    P.op("dve", lambda e, q3=q3, r3=r3: e.tensor_tensor(r3[:, :, 16:32], q3[:, :, 80:96], snb, ALU.mult), [bqk, "sinT"], [rk])
                    P.op("pool", lambda e, o3=o3, r3=r3: e.tensor_tensor(o3[:, :, 64:80], r3[:, :, 0:16], r3[:, :, 16:32], ALU.subtract), [rk], [ok])
                    P.op("dve", lambda e, q3=q3, r3=r3: e.tensor_tensor(r3[:, :, 0:16], q3[:, :, 80:96], csb, ALU.mult), [bqk, "cosT"], [rk])
                    P.op("dve", lambda e, q3=q3, r3=r3: e.tensor_tensor(r3[:, :, 16:32], q3[:, :, 64:80], snb, ALU.mult), [bqk, "sinT"], [rk])
                    P.op("pool", lambda e, o3=o3, r3=r3: e.tensor_tensor(o3[:, :, 80:96], r3[:, :, 0:16], r3[:, :, 16:32], ALU.add), [rk], [ok])
                P.dma(Q1[tt * 128:(tt + 1) * 128, :], stA[sl][:, 0:1536], f"oA{sl}", reads=[("stA", sl)], writes=[("Q1", tt)], eng="pool")
                for kc in range(4):
                    bv, bvk = nxt()
                    P.mm(bv, cnT[:, 256:384], wukv[:, kc * 512:(kc + 1) * 512], True, True,
                         reads=[("cnT1", sl), "wukv"], writes=[bvk])
                    kv3 = bv.rearrange("p (h c) -> p h c", c=128)
                    k3 = stA[sl][:, 1536 + kc * 384: 1536 + (kc + 1) * 384].rearrange("p (h c) -> p h c", c=96)
                    v3 = stV[sl][:, kc * 256:(kc + 1) * 256].rearrange("p (h c) -> p h c", c=64)
                    P.act(k3[:, :, 0:64], kv3[:, :, 0:64], AF.Identity, [bvk], [("stK", sl)])
                    P.op("dve", lambda e, kv3=kv3, v3=v3: e.tensor_copy(v3, kv3[:, :, 64:128]), [bvk], [("stV", sl)])
                k3a = stA[sl][:, 1536:3072].rearrange("p (h c) -> p h c", c=96)
                P.op("pool", lambda e: e.tensor_copy(k3a[:, :, 64:96], krb[:].unsqueeze(1).to_broadcast([128, 16, 32])),
                     [("krb1", sl)], [("stK", sl)])
                P.dma(K1[tt * 128:(tt + 1) * 128, :], stA[sl][:, 1536:3072], f"oK{sl}", reads=[("stK", sl)], writes=[("K1", tt)], eng="pool")
                P.dma(V1[tt * 128:(tt + 1) * 128, :], stV[sl][:], f"oV{sl}", reads=[("stV", sl)], writes=[("V1", tt)], eng="pool")

            def rec1(f, t):
                if t < 0 or t >= NT:
                    return []
                P.rec_start(); f(t, t % 2); return P.rec_stop()

            for step in range(NT + 2):
                P.emit_zipn([rec1(lambda t, s: make_uT(1, X1, t, s), step), rec1(a1_h2, step - 1), rec1(a1_s2b, step - 2)])

            P.release(M0)

        if "B1" in phases:
            qh = P.sb("qh", [128, NT * 96], BF16)
            kh = P.sb("kh", [128, NT * 96], BF16)
            vh = [P.sb(f"vh{i}", [128, NT * 65], BF16) for i in range(2)]
            qTh = [P.sb(f"qTh{i}", [96, S], BF16) for i in range(2)]
            kTh = [P.sb(f"kTh{i}", [96, S], BF16) for i in range(2)]
            pT1 = [P.sb(f"pT1_{i}", [128, 1024], BF16) for i in range(2)]
            ost = [P.sb(f"ost{i}", [128, NT * 65], F32) for i in range(2)]
            q1k = [("Q1", t) for t in range(NT)]
            k1k = [("K1", t) for t in range(NT)]
            v1k = [("V1", t) for t in range(NT)]
            Q1v = Q1.rearrange("(t p) (h c) -> p t h c", p=128, c=96)
            K1v = K1.rearrange("(t p) (h c) -> p t h c", p=128, c=96)
            V1v = V1.rearrange("(t p) (h c) -> p t h c", p=128, c=64)
            A1v = ATT1.rearrange("(t p) (h c) -> p t h c", p=128, c=65)
            sc1 = float(96 ** -0.5)
            for i in range(2):
                P.op("pool", lambda e, i=i: e.memset(vh[i][:], 1.0), [], [("vh", i)])
            def prep_parts(h):
                hs = h % 2
                parts = []

                def loads():
                    P.dma(qh[:].rearrange("p (t c) -> p t c", c=96), Q1v[:, :, h, :], "qh", reads=q1k, writes=["qh"])
                    P.dma(kh[:].rearrange("p (t c) -> p t c", c=96), K1v[:, :, h, :], "kh", reads=k1k, writes=["kh"])
                    P.dma(vh[hs][:].rearrange("p (t c) -> p t c", c=65)[:, :, 0:64], V1v[:, :, h, :], f"vh{hs}", reads=v1k, writes=[("vh", hs)])
                parts.append(loads)
                for src, dstT, key, sk in ((qh, qTh[hs], ("qTh", hs), "qh"), (kh, kTh[hs], ("kTh", hs), "kh")):
                    for t8 in range(4):
                        def chunk(src=src, dstT=dstT, key=key, sk=sk, t8=t8):
                            bank = t8 % 2
                            for j in range(8):
                                t = t8 * 8 + j
                                P.tr(pYb[0:96, bank * 1024 + j * 128: bank * 1024 + (j + 1) * 128], src[:, t * 96:(t + 1) * 96], identb[:],
                                     reads=[sk, "identb"], writes=[("pY", bank)], inc=(j == 7))
                            P.op("dve", lambda e: e.tensor_copy(
                                dstT[:, t8 * 1024:(t8 + 1) * 1024], pYb[0:96, bank * 1024:(bank + 1) * 1024]),
                                [("pY", bank)], [key])
                        parts.append(chunk)
                return parts

            def prep(h):
                for f in prep_parts(h):
                    f()

            its = [(h, qg, kp) for h in range(16) for qg in range(8) for kp in range(16)]

            def qk(i):
                h, qg, kp = its[i]
                hs, ps = h % 2, i % 2
                for j in range(2):
                    kt = kp * 2 + j
                    P.mm(pX[:, ps * 1024 + j * 512: ps * 1024 + (j + 1) * 512], kTh[hs][:, kt * 128:(kt + 1) * 128],
                         qTh[hs][:, qg * 512:(qg + 1) * 512], True, True,
                         reads=[("kTh", hs), ("qTh", hs)], writes=[("pX", ps * 2 + j)], inc=True)

            def ex(i):
                ps = i % 2
                P.act(pT1[ps][:], pX[:, ps * 1024:(ps + 1) * 1024], AF.Exp, [("pX", ps * 2), ("pX", ps * 2 + 1)],
                      [("pT1", ps)], scale=sc1)

            def pv(i):
                h, qg, kp = its[i]
                hs, ps, zs = h % 2, i % 2, qg % 2
                for j in range(2):
                    kt = kp * 2 + j
                    for qs in range(4):
                        P.mm(pZ[:, zs * 512 + qs * 65: zs * 512 + (qs + 1) * 65],
                             pT1[ps][:, j * 512 + qs * 128: j * 512 + (qs + 1) * 128],
                             vh[hs][:, kt * 65:(kt + 1) * 65], (kt == 0 and qs == 0), kt == 31,
                             reads=[("pT1", ps), ("vh", hs)], writes=[("pZ", zs)], inc=(j == 1 and qs == 3))
                if kp == 15:
                    P.op("dve", lambda e: e.tensor_copy(ost[hs][:, qg * 260:(qg + 1) * 260], pZ[:, zs * 512: zs * 512 + 260]),
                         [("pZ", zs)], [("ost", hs)])
                    if qg == 7:
                        P.dma(A1v[:, :, h, :], ost[hs][:].rearrange("p (t c) -> p t c", c=65), f"oB{hs}", reads=[("ost", hs)],
                              writes=[("ATT1", h)], eng="pool")

            prep(0)
            qk(0)
            for i in range(len(its)):
                h, qg, kp = its[i]
                if qg == 2 and kp < 9 and h + 1 < 16:
                    if kp == 0:
                        pend = prep_parts(h + 1)
                    pend[kp]()
                if i + 1 < len(its):
                    qk(i + 1)
                ex(i)
                pv(i)
            P.release(M0)

        if "C1" in phases:
            phase_c(1, 16, WB, ATT1, SG1, X1, out, bwout_in, [("ATT1", h) for h in range(16)],
                    [("SG1", t) for t in range(NT)], [("xsrc", 1, t) for t in range(NT)])

        P.finish()
    return nc


def make_in_maps(inputs):
    f = lambda a: np.ascontiguousarray(np.asarray(a, dtype=np.float32))
    x = f(inputs["x"]); c = f(inputs["c"]); rel_bias = f(inputs["rel_bias"])
    mask, bidx, cos, sin = _const_tables()
    btab = np.zeros((128, 36, 128), np.float32)
    for g in range(3):
        for ty in range(2):
            for h in range(6):
                btab[:, g * 12 + ty * 6 + h, :] = rel_bias[bidx[g, ty], g * 6 + h]
    shared = {
        "ada_w": f(inputs["ada_w"]),
        "ada_b": f(inputs["ada_b"]),
        "ln_g": f(inputs["ln_g"]), "ln_b": f(inputs["ln_b"]),
        "a_w_in": f(inputs["a_w_in"])[0], "a_w_out": f(inputs["a_w_out"])[0],
        "b_w_in": f(inputs["b_w_in"])[0], "b_q_norm": f(inputs["b_q_norm"])[0],
        "b_w_uq": f(inputs["b_w_uq"])[0], "b_kv_norm": f(inputs["b_kv_norm"])[0],
        "b_w_ukv": f(inputs["b_w_ukv"])[0], "b_w_out": f(inputs["b_w_out"])[0],
        "ident": np.eye(128, dtype=np.float32),
        "cosT": np.ascontiguousarray(cos.reshape(NT, 128, 16).transpose(1, 0, 2)),
        "sinT": np.ascontiguousarray(sin.reshape(NT, 128, 16).transpose(1, 0, 2)),
        "maskT": np.ascontiguousarray(mask.transpose(1, 0, 2)),
        "btab": btab,
    }
    maps = []
    for b in range(8):
        m = dict(shared)
        m["x"] = np.ascontiguousarray(x[b])
        m["cT"] = np.ascontiguousarray(c[b].reshape(8, 128).T)
        maps.append(m)
    return maps


_NC_CACHE = {}


def kernel(**inputs):
    if "nc" not in _NC_CACHE:
        _NC_CACHE["nc"] = build_program()
    nc = _NC_CACHE["nc"]
    maps = make_in_maps(inputs)
    res = run_bass_kernel_spmd(nc, maps, core_ids=list(range(8)))
    return np.stack([np.asarray(r["out"], dtype=np.float32) for r in res.results], axis=0)
```
